# Optimizing a Trainium2 kernel written in Bass

```python
import math
import jax
import jax.numpy as jnp
from jax import lax
import numpy as np

D_MODEL = 1024
BATCH = 8
SEQ = 2048
DEPTH = 2

N_MIXERS = 2
N_MLA_LAYERS = (DEPTH + 1) // 2
N_DIL_LAYERS = DEPTH // 2

MLA_HEADS = 16
MLA_NOPE = 64
MLA_ROPE = 32
MLA_QK = MLA_NOPE + MLA_ROPE
MLA_V = 64
Q_LORA = 3 * D_MODEL // 8
KV_LORA = D_MODEL // 4
MLA_IN = Q_LORA + KV_LORA + MLA_ROPE
ROPE_THETA = 10000.0
Q_BLOCK = 128

DIL_GROUPS = ((128, 1), (512, 4), (2048, 16))
N_DIL_GROUPS = len(DIL_GROUPS)
DIL_HEADS = 16
DIL_HEAD_DIM = 64
DIL_IN = N_DIL_GROUPS * 3 * DIL_HEADS * DIL_HEAD_DIM

N_BUCKETS = 32
MAX_DISTANCE = 1024

D_FF = 4 * D_MODEL
N_MOD = 6
EPS = 1e-6

kernel_name = "hybrid_mla_dilated_encoder"


def rms_norm(x, g):
    xf = x.astype(jnp.float32)
    y = xf * lax.rsqrt(jnp.mean(xf * xf, axis=-1, keepdims=True) + EPS)
    return (y * g.astype(jnp.float32)).astype(x.dtype)


def rope(x, pos):
    half = x.shape[-1] // 2
    inv = 1.0 / (ROPE_THETA ** (jnp.arange(half, dtype=jnp.float32) / half))
    ang = pos.astype(jnp.float32)[..., None] * inv
    ang = ang.reshape(ang.shape[:2] + (1,) * (x.ndim - 3) + (half,))
    cos, sin = jnp.cos(ang), jnp.sin(ang)
    x1 = x[..., :half].astype(jnp.float32)
    x2 = x[..., half:].astype(jnp.float32)
    out = jnp.concatenate([x1 * cos - x2 * sin, x2 * cos + x1 * sin], axis=-1)
    return out.astype(x.dtype)


def t5_bucket(rel):
    nb = N_BUCKETS // 2
    max_exact = nb // 2
    ret = jnp.where(rel > 0, nb, 0)
    n = jnp.abs(rel)
    nf = jnp.maximum(n, 1).astype(jnp.float32)
    large = max_exact + (jnp.log(nf / max_exact) / math.log(MAX_DISTANCE / max_exact)
                         * (nb - max_exact)).astype(jnp.int32)
    large = jnp.minimum(large, nb - 1)
    return ret + jnp.where(n < max_exact, n, large)


def dense_attention(q, k, v, scale):
    B, S, H, Dk = q.shape
    nq = S // Q_BLOCK
    qb = jnp.moveaxis(q.reshape(B, nq, Q_BLOCK, H, Dk), 1, 0)

    def block(qi):
        s = jnp.einsum('bqhd,bkhd->bhqk', qi, k, preferred_element_type=jnp.float32) * scale
        p = jax.nn.softmax(s, axis=-1).astype(v.dtype)
        return jnp.einsum('bhqk,bkhd->bqhd', p, v)

    o = lax.map(block, qb)
    return jnp.moveaxis(o, 0, 1).reshape(B, S, H, v.shape[-1])


def mla_mixer(h, pos, w_in, g_qa, w_qb, g_kva, w_kvb, g_q, g_k, w_o):
    B, S, _ = h.shape
    proj = h @ w_in
    q_lat = proj[..., :Q_LORA]
    kv_lat = proj[..., Q_LORA:Q_LORA + KV_LORA]
    k_rope = rope(proj[..., Q_LORA + KV_LORA:], pos)
    q = (rms_norm(q_lat, g_qa) @ w_qb).reshape(B, S, MLA_HEADS, MLA_QK)
    kv = (rms_norm(kv_lat, g_kva) @ w_kvb).reshape(B, S, MLA_HEADS, MLA_NOPE + MLA_V)
    q = jnp.concatenate([q[..., :MLA_NOPE], rope(q[..., MLA_NOPE:], pos)], axis=-1)
    k = jnp.concatenate([kv[..., :MLA_NOPE],
                         jnp.broadcast_to(k_rope[:, :, None, :], (B, S, MLA_HEADS, MLA_ROPE))],
                        axis=-1)
    v = kv[..., MLA_NOPE:]
    q = rms_norm(q, g_q)
    k = rms_norm(k, g_k)
    o = dense_attention(q, k, v, MLA_QK ** -0.5)
    return o.reshape(B, S, MLA_HEADS * MLA_V) @ w_o


def _to_residues(t, dilation):
    B, S = t.shape[:2]
    L = S // dilation
    t = t.reshape((B, L, dilation) + t.shape[2:])
    t = jnp.moveaxis(t, 2, 1)
    return t.reshape((B * dilation, L) + t.shape[3:])


def _from_residues(t, batch, dilation):
    L = t.shape[1]
    t = t.reshape((batch, dilation, L) + t.shape[2:])
    t = jnp.moveaxis(t, 1, 2)
    return t.reshape((batch, L * dilation) + t.shape[3:])


def dilated_group(q, k, v, pos, bias_table, window, dilation):
    B, S, H, Dh = q.shape
    half = window // (2 * dilation)
    blk = half
    L = S // dilation
    nb = -(-L // blk)
    Lp = nb * blk
    padq = Lp - L
    N = B * dilation
    qr, kr, vr, pr = (_to_residues(t, dilation) for t in (q, k, v, pos))
    qb = jnp.pad(qr, ((0, 0), (0, padq), (0, 0), (0, 0))).reshape(N, nb, blk, H, Dh)

    def key_blocks(t):
        pad = [(0, 0), (blk, padq + blk)] + [(0, 0)] * (t.ndim - 2)
        t = jnp.pad(t, pad).reshape((N, nb + 2, blk) + t.shape[2:])
        return jnp.concatenate([t[:, :-2], t[:, 1:-1], t[:, 2:]], axis=2)

    kb, vb, pk = key_blocks(kr), key_blocks(vr), key_blocks(pr)
    pq = jnp.pad(pr, ((0, 0), (0, padq))).reshape(N, nb, blk)
    qi = jnp.arange(nb)[:, None] * blk + jnp.arange(blk)[None, :]
    ki = jnp.arange(nb)[:, None] * blk - blk + jnp.arange(3 * blk)[None, :]
    valid = ((ki[:, None, :] >= 0) & (ki[:, None, :] < L)
             & (jnp.abs(qi[:, :, None] - ki[:, None, :]) <= half))
    bucket = t5_bucket(pk[:, :, None, :] - pq[:, :, :, None])
    bias = jnp.moveaxis(bias_table[bucket], -1, 2).astype(jnp.float32)
    s = jnp.einsum('nbqhd,nbkhd->nbhqk', qb, kb,
                   preferred_element_type=jnp.float32) * (DIL_HEAD_DIM ** -0.5) + bias
    s = jnp.where(valid[None, :, None], s, -jnp.inf)
    lse = jax.nn.logsumexp(s, axis=-1)
    p = jnp.exp(s - lse[..., None]).astype(v.dtype)
    o = jnp.einsum('nbhqk,nbkhd->nbqhd', p, vb).reshape(N, Lp, H, Dh)[:, :L]
    lse = jnp.moveaxis(lse, 2, 3).reshape(N, Lp, H)[:, :L]
    return _from_residues(o, B, dilation), _from_residues(lse, B, dilation)


def dilated_mixer(h, pos, w_in, g_q, g_k, rel_bias, w_o):
    B, S, _ = h.shape
    proj = (h @ w_in).reshape(B, S, N_DIL_GROUPS, 3, DIL_HEADS, DIL_HEAD_DIM)
    outs, lses = [], []
    for gi, (window, dilation) in enumerate(DIL_GROUPS):
        q = rms_norm(proj[:, :, gi, 0], g_q[gi])
        k = rms_norm(proj[:, :, gi, 1], g_k[gi])
        v = proj[:, :, gi, 2]
        table = rel_bias[:, gi * DIL_HEADS:(gi + 1) * DIL_HEADS]
        o, lse = dilated_group(q, k, v, pos, table, window, dilation)
        outs.append(o)
        lses.append(lse)
    o_all = jnp.stack(outs, axis=0)
    wts = jax.nn.softmax(jnp.stack(lses, axis=0), axis=0)
    o = jnp.einsum('gbsh,gbshd->bshd', wts.astype(o_all.dtype), o_all)
    return o.reshape(B, S, DIL_HEADS * DIL_HEAD_DIM) @ w_o


def setup_inputs(seed: int = 0) -> dict:
    key = jax.random.key(seed)
    ks = jax.random.split(key, 24)
    f32 = jnp.float32
    nrm = lambda k, shape, s: jax.random.normal(k, shape, f32) * s
    gain = lambda k, shape: 1.0 + 0.05 * jax.random.normal(k, shape, f32)
    D = D_MODEL
    return {
        "x": nrm(ks[0], (BATCH, SEQ, D), 1.0),
        "c": nrm(ks[1], (BATCH, D), 1.0),
        "positions": jnp.broadcast_to(jnp.arange(SEQ, dtype=jnp.int32), (BATCH, SEQ)),
        "ada_w": nrm(ks[2], (DEPTH, D, N_MOD * D), 0.5 * D ** -0.5),
        "ada_b": nrm(ks[3], (DEPTH, N_MOD * D), 0.02),
        "norm_mix": gain(ks[4], (DEPTH, D)),
        "norm_mlp": gain(ks[5], (DEPTH, D)),
        "mlp_w1": nrm(ks[6], (DEPTH, D, D_FF), D ** -0.5),
        "mlp_w2": nrm(ks[7], (DEPTH, D_FF, D), D_FF ** -0.5),
        "mla_w_in": nrm(ks[8], (N_MLA_LAYERS, D, MLA_IN), D ** -0.5),
        "mla_g_qa": gain(ks[9], (N_MLA_LAYERS, Q_LORA)),
        "mla_w_qb": nrm(ks[10], (N_MLA_LAYERS, Q_LORA, MLA_HEADS * MLA_QK), Q_LORA ** -0.5),
        "mla_g_kva": gain(ks[11], (N_MLA_LAYERS, KV_LORA)),
        "mla_w_kvb": nrm(ks[12], (N_MLA_LAYERS, KV_LORA, MLA_HEADS * (MLA_NOPE + MLA_V)), KV_LORA ** -0.5),
        "mla_g_q": gain(ks[13], (N_MLA_LAYERS, MLA_QK)),
        "mla_g_k": gain(ks[14], (N_MLA_LAYERS, MLA_QK)),
        "mla_w_o": nrm(ks[15], (N_MLA_LAYERS, MLA_HEADS * MLA_V, D), (MLA_HEADS * MLA_V) ** -0.5),
        "dil_w_in": nrm(ks[16], (N_DIL_LAYERS, D, DIL_IN), D ** -0.5),
        "dil_g_q": gain(ks[17], (N_DIL_LAYERS, N_DIL_GROUPS, DIL_HEAD_DIM)),
        "dil_g_k": gain(ks[18], (N_DIL_LAYERS, N_DIL_GROUPS, DIL_HEAD_DIM)),
        "dil_w_o": nrm(ks[19], (N_DIL_LAYERS, DIL_HEADS * DIL_HEAD_DIM, D), (DIL_HEADS * DIL_HEAD_DIM) ** -0.5),
        "rel_bias": nrm(ks[20], (N_BUCKETS, N_DIL_GROUPS * DIL_HEADS), 0.3),
    }


def reference(x, c, positions, ada_w, ada_b, norm_mix, norm_mlp, mlp_w1, mlp_w2,
              mla_w_in, mla_g_qa, mla_w_qb, mla_g_kva, mla_w_kvb, mla_g_q, mla_g_k, mla_w_o,
              dil_w_in, dil_g_q, dil_g_k, dil_w_o, rel_bias):
    B, S, D = x.shape
    cond = jax.nn.silu(c)
    for i in range(DEPTH):
        mod = (cond @ ada_w[i] + ada_b[i]).reshape(B, N_MOD, D)
        shift1, scale1, gate1 = mod[:, 0, None, :], mod[:, 1, None, :], mod[:, 2, None, :]
        shift2, scale2, gate2 = mod[:, 3, None, :], mod[:, 4, None, :], mod[:, 5, None, :]
        h = rms_norm(x, norm_mix[i]) * (1 + scale1) + shift1
        j = i // N_MIXERS
        if i % N_MIXERS == 0:
            y = mla_mixer(h, positions, mla_w_in[j], mla_g_qa[j], mla_w_qb[j], mla_g_kva[j],
                          mla_w_kvb[j], mla_g_q[j], mla_g_k[j], mla_w_o[j])
        else:
            y = dilated_mixer(h, positions, dil_w_in[j], dil_g_q[j], dil_g_k[j], rel_bias, dil_w_o[j])
        x = x + gate1 * y
        h = rms_norm(x, norm_mlp[i]) * (1 + scale2) + shift2
        x = x + gate2 * (jnp.square(jax.nn.relu(h @ mlp_w1[i])) @ mlp_w2[i])
    return x
```

```python
import contextlib
import math
import numpy as np
import concourse.bass as bass
import concourse.mybir as mybir
from concourse.bass_utils import run_bass_kernel_spmd

F32 = mybir.dt.float32
BF16 = mybir.dt.bfloat16
I32 = mybir.dt.int32
AF = mybir.ActivationFunctionType
ALU = mybir.AluOpType
AX = mybir.AxisListType

PE, ACT, DVE, POOL, SP = "pe", "act", "dve", "pool", "sp"
ENGS = [PE, ACT, DVE, POOL, SP]

S = 2048
D = 1024
NT = 16
EPS = 1e-6
DIL = ((128, 1), (512, 4), (2048, 16))


class Buf:
    __slots__ = ("name", "last_write", "readers", "dma_sem", "dma_cnt", "last_dma", "excl")

    _n = [0]

    def __init__(self, name, excl=False):
        self.excl = excl
        Buf._n[0] += 1
        self.name = "%s_%d" % (name, Buf._n[0])
        self.last_write = None
        self.readers = []
        self.dma_sem = None
        self.dma_cnt = 0
        self.last_dma = None


class Op:
    __slots__ = ("eng", "fn", "deps", "is_dma", "needs_inc", "ordinal", "dma_buf", "dma_val")

    def __init__(self, eng, fn, is_dma):
        self.eng = eng
        self.fn = fn
        self.deps = []
        self.is_dma = is_dma
        self.needs_inc = False
        self.ordinal = None
        self.dma_buf = None
        self.dma_val = None


class Prog:
    def __init__(self, nc):
        self.nc = nc
        self.ops = {e: [] for e in ENGS}
        self.dma_bufs = []
        self.fence = {e: [] for e in ENGS}

    def _add_dep(self, op, prod):
        if prod is None or prod is op:
            return
        if (not prod.is_dma) and prod.eng == op.eng and prod.eng == PE:
            return
        op.deps.append(prod)
        if not prod.is_dma:
            prod.needs_inc = True

    def op(self, eng, fn, reads=(), writes=(), dma_dst=None):
        o = Op(eng, fn, dma_dst is not None)
        if self.fence[eng]:
            for p in self.fence[eng]:
                if p.is_dma or p.eng != eng:
                    self._add_dep(o, p)
            self.fence[eng] = []
        for b in reads:
            self._add_dep(o, b.last_write)
            if b.excl:
                for r in b.readers:
                    if r.eng != eng:
                        self._add_dep(o, r)
        for b in writes:
            self._add_dep(o, b.last_write)
            for r in b.readers:
                self._add_dep(o, r)
        if dma_dst is not None:
            if dma_dst.dma_sem is None:
                dma_dst.dma_sem = "pending"
                self.dma_bufs.append(dma_dst)
            dma_dst.dma_cnt += 1
            o.dma_buf = dma_dst
            o.dma_val = 16 * dma_dst.dma_cnt
            dma_dst.last_dma = o
        for b in reads:
            b.readers.append(o)
        for b in writes:
            b.last_write = o
            b.readers = []
        self.ops[eng].append(o)
        return o

    def barrier(self):
        f = []
        for e in ENGS:
            last = None
            for o in reversed(self.ops[e]):
                if not o.is_dma:
                    last = o
                    break
            if last is not None:
                f.append(last)
        for b in self.dma_bufs:
            if b.last_dma is not None:
                f.append(b.last_dma)
        self.fence = {e: list(f) for e in ENGS}

    def emit(self, final_waits=()):
        nc = self.nc
        with contextlib.ExitStack() as st:
            sems = {e: st.enter_context(nc.semaphore("s_" + e)) for e in ENGS}
            for b in self.dma_bufs:
                b.dma_sem = st.enter_context(nc.semaphore("d_" + b.name))
            for e in ENGS:
                n = 0
                for o in self.ops[e]:
                    if o.needs_inc and not o.is_dma:
                        n += 1
                        o.ordinal = n
            block = st.enter_context(nc.Block())
            prog = self

            def run(e, h):
                waited = {}
                for o in prog.ops[e]:
                    for p in o.deps:
                        if p.is_dma:
                            s, v = p.dma_buf.dma_sem, p.dma_val
                        else:
                            s, v = sems[p.eng], p.ordinal
                        key = id(s)
                        if waited.get(key, 0) >= v:
                            continue
                        waited[key] = v
                        h.wait_ge(s, v)
                    ins = o.fn(h)
                    if o.is_dma:
                        ins.then_inc(o.dma_buf.dma_sem, 16)
                    elif o.needs_inc:
                        ins.then_inc(sems[e], 1)
                if e == SP:
                    for b in final_waits:
                        h.wait_ge(b.dma_sem, 16 * b.dma_cnt)

            @block.tensor
            def _(h):
                run(PE, h)

            @block.scalar
            def _(h):
                run(ACT, h)

            @block.vector
            def _(h):
                run(DVE, h)

            @block.gpsimd
            def _(h):
                run(POOL, h)

            @block.sync
            def _(h):
                run(SP, h)


def _t5_bucket_np(rel):
    n = np.abs(rel)
    thr = [15, 27, 50, 91, 166, 305, 559]
    mag = np.minimum(n, 8) + sum((n >= t).astype(np.int64) for t in thr)
    return (rel > 0) * 16 + mag


def _onehot_const():
    oh = np.zeros((33, 3, 384), np.float32)
    for g, (_, d) in enumerate(DIL):
        for m in range(384):
            j = 191 - m
            if abs(j) <= 64:
                oh[_t5_bucket_np(np.array([j * d]))[0], g, m] = 1.0
            else:
                oh[32, g, m] = 1.0
    return oh.reshape(33, 3 * 384)


def _invf_const():
    half = 16
    inv = 1.0 / (10000.0 ** (np.arange(half, dtype=np.float64) / half))
    return np.ascontiguousarray(np.broadcast_to((inv / (2 * math.pi)).astype(np.float32), (128, 16)))


def _windows(d):
    L = S // d
    nkt = L // 128
    banks = [[] for _ in range(4)]
    for r in range(d):
        for jt in range(nkt):
            lo = max(0, 128 * jt - 64)
            hi = min(L, 128 * jt + 192)
            a = r * L + lo
            b = r * L + hi
            while a < b:
                bank = a // 512
                e = min(b, (bank + 1) * 512)
                q_in = a - r * L
                eoff = q_in - (128 * jt - 64)
                banks[bank].append((r * nkt + jt, a, e, eoff))
                a = e
    return banks


class _Stop(Exception):
    pass


class Builder:
    def __init__(self, layers, stop=None, dbg=""):
        self.layers = tuple(layers)
        self.stop = stop
        self.dbg = dbg
        nc = self.nc = bass.Bass("TRN2", target_bir_lowering=False)
        self.P = Prog(nc)
        dt = nc.dram_tensor
        inp = lambda name, shape, ty=F32: dt(name, shape, ty, kind="ExternalInput").ap()
        self.x = inp("x", [S, D])
        self.c = inp("c", [1, D])
        self.pos = inp("pos", [16, 128], I32)
        self.ada_w = inp("ada_w", [2, D, 6 * D])
        self.ada_b = inp("ada_b", [2, 6 * D])
        self.norm_mix = inp("norm_mix", [2, D])
        self.norm_mlp = inp("norm_mlp", [2, D])
        self.w1 = inp("mlp_w1", [2, D, 4 * D])
        self.w2 = inp("mlp_w2", [2, 4 * D, D])
        self.mla_w_in = inp("mla_w_in", [D, 672])
        self.mla_g_lat = inp("mla_g_lat", [1, 640])
        self.mla_w_qb = inp("mla_w_qb", [384, 1536])
        self.mla_w_kvb = inp("mla_w_kvb", [256, 2048])
        self.mla_g_q = inp("mla_g_q", [1, 96])
        self.mla_g_k = inp("mla_g_k", [1, 96])
        self.mla_w_o = inp("mla_w_o", [D, D])
        self.dil_w_in = inp("dil_w_in", [D, 9216])
        self.dil_g_q = inp("dil_g_q", [1, 192])
        self.dil_g_k = inp("dil_g_k", [1, 192])
        self.dil_w_o = inp("dil_w_o", [D, D])
        self.rel_bias = inp("rel_bias", [32, 48])
        self.oh = inp("oh", [33, 1152])
        self.invf = inp("invf", [128, 16])
        self.out = dt("out", [S, D], F32, kind="ExternalOutput").ap()
        self.zscr = dt("zscr", [48 * 128 * 640], BF16, kind="Internal")

    def view(self, off, nbytes, dtype=F32, parts=128):
        assert off % 4 == 0 and nbytes % 4 == 0 and off + nbytes <= self.ARENA, (off, nbytes)
        a = self.arena[0:parts, off // 4:(off + nbytes) // 4]
        return a if dtype == F32 else a.bitcast(dtype)

    class Region:
        def __init__(self, b, start, end):
            self.b, self.start, self.end, self.cur = b, start, end, start

        def take(self, nbytes, dtype=F32):
            nbytes = (nbytes + 31) // 32 * 32
            assert self.cur + nbytes <= self.end, ("region overflow", self.cur, nbytes, self.end)
            v = self.b.view(self.cur, nbytes, dtype)
            self.cur += nbytes
            return v

        def reset(self):
            self.cur = self.start

    def mm(self, out, lhsT, rhs, start, stop, reads, writes, skip=False):
        self.P.op(PE, lambda e: e.matmul(out, lhsT=lhsT, rhs=rhs, start=start, stop=stop,
                                         skip_group_check=skip), reads, writes)

    def tr(self, out, in_, ident, reads, writes):
        self.P.op(PE, lambda e: e.transpose(out=out, in_=in_, identity=ident), reads, writes)

    def act(self, out, in_, func, reads, writes, scale=None, bias=None, accum=None):
        kw = {}
        if scale is not None:
            kw["scale"] = scale
        if bias is not None:
            kw["bias"] = bias
        if accum is not None:
            kw["accum_out"] = accum
        self.P.op(ACT, lambda e: e.activation(out=out, in_=in_, func=func, **kw), reads, writes)

    def tt(self, eng, out, in0, in1, op, reads, writes):
        self.P.op(eng, lambda e: e.tensor_tensor(out=out, in0=in0, in1=in1, op=op), reads, writes)

    def ts(self, eng, out, in0, s1, s2, op0, op1, reads, writes):
        if s2 is None:
            self.P.op(eng, lambda e: e.tensor_scalar(out=out, in0=in0, scalar1=s1, scalar2=None, op0=op0),
                      reads, writes)
        else:
            self.P.op(eng, lambda e: e.tensor_scalar(out=out, in0=in0, scalar1=s1, scalar2=s2, op0=op0, op1=op1),
                      reads, writes)

    def stt(self, eng, out, in0, scalar, in1, op0, op1, reads, writes):
        self.P.op(eng, lambda e: e.scalar_tensor_tensor(out=out, in0=in0, scalar=scalar, in1=in1, op0=op0, op1=op1),
                  reads, writes)

    def cp(self, eng, out, in_, reads, writes):
        self.P.op(eng, lambda e: e.tensor_copy(out=out, in_=in_), reads, writes)

    def memset(self, eng, ap, val, writes):
        self.P.op(eng, lambda e: e.memset(ap, val), (), writes)

    def recip(self, out, in_, reads, writes):
        self.P.op(DVE, lambda e: e.reciprocal(out=out, in_=in_), reads, writes)

    def reduce(self, out, in_, reads, writes):
        self.P.op(DVE, lambda e: e.tensor_reduce(out=out, in_=in_, axis=AX.X, op=ALU.add), reads, writes)

    def dma(self, eng, out, in_, reads, writes, dst):
        self.P.op(eng, lambda e: e.dma_start(out=out, in_=in_), reads, writes, dma_dst=dst)

    def row_to_cols(self, ps_ap, ps_buf, row_ap, row_buf, n, m=128, col0=0):
        for j in range(n // m):
            self.mm(ps_ap[0:m, col0 + j:col0 + j + 1], row_ap[0:1, j * m:(j + 1) * m], self.one[0:1, 0:1],
                    True, True, [row_buf, self.cb], [ps_buf])

    def build(self):
        nc = self.nc
        self.ARENA = 212000
        with contextlib.ExitStack() as st:
            self.arena = st.enter_context(nc.sbuf_tensor("arena", [128, self.ARENA // 4], F32))
            self.ps = [st.enter_context(nc.psum_tensor("ps%d" % i, [128, 512], F32)) for i in range(8)]
            self.pb = [Buf("ps%d" % i, excl=True) for i in range(8)]
            self.psb = [self.ps[i][:, :].bitcast(BF16) for i in range(8)]
            K = 1024
            self.X = self.view(0, 64 * K).rearrange("p (t n) -> p t n", t=NT)
            self.Xb = [Buf("X%d" % t) for t in range(NT)]
            self.HTr = Builder.Region(self, 64 * K, 96 * K)
            self.HT = self.view(64 * K, 32 * K, BF16).rearrange("p (c n) -> p c n", c=8)
            self.HTb = Buf("HT")
            self.gates = self.view(96 * K, 8 * K).rearrange("p (g n) -> p g n", g=2)
            self.gb = Buf("gates")
            cr = Builder.Region(self, 104 * K, 107 * K)
            self.identb = cr.take(256, BF16)
            self.identf = cr.take(512)
            self.ones = cr.take(512)
            self.one = self.ones
            self.cols = cr.take(128)
            self.epsc = cr.take(32)
            self.ss = cr.take(64)
            self.rstd = cr.take(64)
            self.condT = cr.take(32, BF16)
            self.cb = Buf("consts")
            self.colsb = Buf("cols")
            self.statb = Buf("stat")
            self.R = Builder.Region(self, 107 * K, self.ARENA)
            self.NS = Builder.Region(self, self.ARENA - 8192, self.ARENA)
            self.outb = Buf("out")

            try:
                self.setup()
                self.chk(1)
                for li in self.layers:
                    self.ada(li)
                    self.P.barrier()
                    self.chk(2)
                    self.norm_to_ht(0)
                    self.chk(3)
                    if li == 0:
                        self.mla_layer()
                    else:
                        self.dil_layer()
                    self.P.barrier()
                    self.chk(6)
                    self.norm_to_ht(16)
                    self.mlp(li)
                    self.P.barrier()
            except _Stop:
                self.P.barrier()
            for t in range(NT):
                self.dma(SP, self.out[t * 128:(t + 1) * 128, :], self.X[:, t, :], [self.Xb[t]], [self.outb], self.outb)
            self.P.emit(final_waits=[self.outb])
        return nc

    def chk(self, stage):
        if self.stop is not None and self.stop == stage:
            raise _Stop()

    def setup(self):
        for t in range(NT):
            self.dma(SP, self.X[:, t, :], self.x[t * 128:(t + 1) * 128, :], [], [self.Xb[t]], self.Xb[t])
        cb = self.cb
        self.memset(POOL, self.identf, 0.0, [cb])
        self.P.op(POOL, lambda e: e.affine_select(out=self.identf, in_=self.identf, pattern=[[-1, 128]],
                                                  compare_op=ALU.not_equal, fill=1.0, base=0,
                                                  channel_multiplier=1), [cb], [cb])
        self.cp(POOL, self.identb, self.identf, [cb], [cb])
        self.memset(POOL, self.ones, 1.0, [cb])
        self.memset(POOL, self.epsc, EPS, [cb])
        self.R.reset()
        crow = self.R.take(4096)
        crb = Buf("crow")
        self.dma(SP, crow[0:1, :], self.c[0:1, :], [], [crb], crb)
        self.act(crow[0:1, :], crow[0:1, :], AF.Silu, [crb], [crb])
        self.row_to_cols(self.ps[7], self.pb[7], crow, crb, 1024)
        self.cp(DVE, self.condT[:, 0:8], self.ps[7][:, 0:8], [self.pb[7]], [cb])
        self.P.barrier()

    def ada(self, li):
        self.R.reset()
        self.HTr.reset()
        modrow = self.HTr.take(24576)
        nmrow = self.HTr.take(8192)
        adab = self.R.take(24576)
        wa = [self.R.take(8 * 1536 * 2, BF16).rearrange("p (k n) -> p k n", k=8) for _ in range(2)]
        wab = [Buf("wa0"), Buf("wa1")]
        mb, nb, ab = Buf("modrow"), Buf("nmrow"), Buf("adab")
        self.dma(SP, adab[0:1, :], self.ada_b[li:li + 1, :], [], [ab], ab)
        self.dma(SP, nmrow[0:1, 0:1024], self.norm_mix[li:li + 1, :], [], [nb], nb)
        self.dma(SP, nmrow[0:1, 1024:2048], self.norm_mlp[li:li + 1, :], [], [nb], nb)
        aw = self.ada_w[li].rearrange("(k p) n -> p k n", p=128)
        for blk in range(4):
            s = blk % 2
            for k in range(8):
                self.dma(POOL, wa[s][:, k, :], aw[:, k, blk * 1536:(blk + 1) * 1536], [], [wab[s]], wab[s])
            for cg in range(3):
                pbank = 5 + (blk * 3 + cg) % 2
                for k in range(8):
                    self.mm(self.ps[pbank][0:1, :], self.condT[:, k:k + 1], wa[s][:, k, cg * 512:(cg + 1) * 512],
                            k == 0, k == 7, [wab[s], self.cb], [self.pb[pbank]])
                c0 = blk * 1536 + cg * 512
                self.tt(DVE, modrow[0:1, c0:c0 + 512], self.ps[pbank][0:1, :], adab[0:1, c0:c0 + 512], ALU.add,
                        [self.pb[pbank], ab], [mb])
        sl = lambda i: modrow[0:1, i * 1024:(i + 1) * 1024]
        self.stt(DVE, sl(1), sl(1), 1.0, nmrow[0:1, 0:1024], ALU.add, ALU.mult, [mb, nb], [mb])
        self.stt(DVE, sl(4), sl(4), 1.0, nmrow[0:1, 1024:2048], ALU.add, ALU.mult, [mb, nb], [mb])
        p7, pb7 = self.ps[7], self.pb[7]
        for j, src in enumerate((1, 0, 4, 3)):
            self.row_to_cols(p7, pb7, sl(src), mb, 1024, col0=8 * j)
        self.cp(DVE, self.cols[:, 0:32], p7[:, 0:32], [pb7], [self.colsb])
        for gi, src in enumerate((2, 5)):
            for hlf in range(2):
                pbank = 5 + hlf
                self.mm(self.ps[pbank][:, :], self.ones[0:1, 0:128], sl(src)[0:1, hlf * 512:(hlf + 1) * 512],
                        True, True, [mb, self.cb], [self.pb[pbank]])
                self.cp(DVE, self.gates[:, gi, hlf * 512:(hlf + 1) * 512], self.ps[pbank][:, :],
                        [self.pb[pbank]], [self.gb])

    def norm_to_ht(self, col0):
        self.NS.reset()
        junk = self.NS.take(4096)
        jb = Buf("junk")
        xn = [self.NS.take(2048, BF16) for _ in range(2)]
        xnb = [Buf("xn0"), Buf("xn1")]
        sb = self.statb
        self.memset(DVE, self.ss, 0.0, [sb])
        for t in range(NT):
            self.act(junk, self.X[:, t, :], AF.Square, [self.Xb[t]], [jb, sb], accum=self.ss[:, t:t + 1])
        self.act(self.rstd, self.ss, AF.Sqrt, [sb, self.cb], [sb], scale=1.0 / D, bias=self.epsc[:, 0:1])
        self.recip(self.rstd, self.rstd, [sb], [sb])
        tp = self.psb[2].rearrange("p (c n) -> p c n", c=8)
        tpb = self.pb[2]
        tp2 = self.psb[3].rearrange("p (c n) -> p c n", c=8)
        tps, tpbs = [tp, tp2], [self.pb[2], self.pb[3]]
        for t in range(NT):
            s = t % 2
            self.ts(DVE, xn[s], self.X[:, t, :], self.rstd[:, t:t + 1], None, ALU.mult, None,
                    [self.Xb[t], sb], [xnb[s]])
            for c in range(8):
                self.tr(tps[s][:, c, :], xn[s][:, c * 128:(c + 1) * 128], self.identb, [xnb[s], self.cb], [tpbs[s]])
            for c in range(8):
                o = self.HT[:, c, t * 128:(t + 1) * 128]
                sc = self.cols[:, col0 + c:col0 + c + 1]
                bi = self.cols[:, col0 + 8 + c:col0 + 8 + c + 1]
                if s == 0:
                    self.act(o, tps[s][:, c, :], AF.Identity, [tpbs[s], self.colsb], [self.HTb], scale=sc, bias=bi)
                else:
                    self.ts(DVE, o, tps[s][:, c, :], sc, bi, ALU.mult, ALU.add, [tpbs[s], self.colsb], [self.HTb])

    def resid_update(self, t, gi, banks, ytmp, ytb):
        for hlf in range(2):
            self.tt(DVE, ytmp[:, hlf * 512:(hlf + 1) * 512], self.ps[banks[hlf]][:, :],
                    self.gates[:, gi, hlf * 512:(hlf + 1) * 512], ALU.mult, [self.pb[banks[hlf]], self.gb], [ytb])
        self.tt(POOL, self.X[:, t, :], self.X[:, t, :], ytmp, ALU.add, [ytb, self.Xb[t]], [self.Xb[t]])

    def mlp(self, li):
        self.R.reset()
        R = self.R
        w1 = [R.take(8 * 1024 * 2, BF16).rearrange("p (k n) -> p k n", k=8) for _ in range(2)]
        w2 = [R.take(8 * 1024 * 2, BF16).rearrange("p (k n) -> p k n", k=8) for _ in range(2)]
        w1b = [Buf("w1_0"), Buf("w1_1")]
        w2b = [Buf("w2_0"), Buf("w2_1")]
        hid = [R.take(8 * 512 * 2, BF16).rearrange("p (k n) -> p k n", k=8) for _ in range(2)]
        hidb = [Buf("hid0"), Buf("hid1")]
        rs = [R.take(2048) for _ in range(2)]
        rsb = [Buf("rs0"), Buf("rs1")]
        ytmp = [R.take(4096) for _ in range(2)]
        ytb = [Buf("yt0"), Buf("yt1")]
        W1 = self.w1[li].rearrange("(k p) n -> p k n", p=128)
        W2 = self.w2[li].rearrange("(k p) n -> p k n", p=128)
        cnt = 0
        ycnt = 0
        for qh in range(4):
            s = qh % 2
            for k in range(8):
                self.dma(POOL, w1[s][:, k, :], W1[:, k, qh * 1024:(qh + 1) * 1024], [], [w1b[s]], w1b[s])
            for k in range(8):
                self.dma(POOL, w2[s][:, k, :], W2[:, qh * 8 + k, :], [], [w2b[s]], w2b[s])
            for tg in range(4):
                hs = tg % 2
                for hc in range(8):
                    pbk = cnt % 2
                    cnt += 1
                    for k in range(8):
                        self.mm(self.ps[pbk][:, :], w1[s][:, k, hc * 128:(hc + 1) * 128],
                                self.HT[:, k, tg * 512:(tg + 1) * 512], k == 0, k == 7,
                                [w1b[s], self.HTb], [self.pb[pbk]])
                    self.act(rs[pbk], self.ps[pbk][:, :], AF.Relu, [self.pb[pbk]], [rsb[pbk]])
                    self.tt(DVE, hid[hs][:, hc, :], rs[pbk], rs[pbk], ALU.mult, [rsb[pbk]], [hidb[hs]])
                for tt_ in range(4):
                    t = tg * 4 + tt_
                    ys = ycnt % 2
                    ycnt += 1
                    banks = (2 + 2 * ys, 3 + 2 * ys)
                    for hlf in range(2):
                        for hc in range(8):
                            self.mm(self.ps[banks[hlf]][:, :], hid[hs][:, hc, tt_ * 128:(tt_ + 1) * 128],
                                    w2[s][:, hc, hlf * 512:(hlf + 1) * 512], hc == 0, hc == 7,
                                    [hidb[hs], w2b[s]], [self.pb[banks[hlf]]])
                    self.resid_update(t, 1, banks, ytmp[ys], ytb[ys])

    def out_proj(self, ont, ontb, extra, wo, wob, ytmp, ytb):
        for t in range(NT):
            ys = t % 2
            banks = (0, 1) if ys == 0 else (2, 3)
            for hlf in range(2):
                for h in range(4):
                    self.mm(self.ps[banks[hlf]][:, :], ont[0:64, h, t * 128:(t + 1) * 128],
                            wo[0:64, h, hlf * 512:(hlf + 1) * 512], h == 0, h == 3,
                            [ontb, wob] + extra, [self.pb[banks[hlf]]])
            self.resid_update(t, 0, banks, ytmp[ys], ytb[ys])

    def normalize_heads(self, src_fn, src_bufs, ont, ontb, extra_w, rrow, rrb):
        for h in range(4):
            for ch in range(4):
                num, den = src_fn(h, ch)
                self.recip(rrow[64:65, 0:512], den, src_bufs, [rrb])
                self.mm(self.ps[7][0:64, :], self.ones[64:65, 0:64], rrow[64:65, 0:512], True, True,
                        [rrb, self.cb], [self.pb[7]])
                self.tt(DVE, ont[0:64, h, ch * 512:(ch + 1) * 512], num, self.ps[7][0:64, :], ALU.mult,
                        src_bufs + [self.pb[7]], [ontb] + extra_w)

    def mla_layer(self):
        R = self.R
        R.reset()
        P = self.P
        latt = R.take(5 * 2048 * 2, BF16).rearrange("p (j n) -> p j n", j=5)
        lattb = Buf("latt")
        epsq = R.take(64)
        rkv = R.take(64)
        ssq = R.take(64)
        sskv = R.take(64)
        kr = R.take(16 * 32 * 4).rearrange("p (t n) -> p t n", t=NT)
        krr = R.take(16 * 32 * 4).rearrange("p (t n) -> p t n", t=NT)
        cos = R.take(16 * 16 * 4).rearrange("p (t n) -> p t n", t=NT)
        sin = R.take(16 * 16 * 4).rearrange("p (t n) -> p t n", t=NT)
        gqk = R.take(32)
        stb, krb, csb, gqb = Buf("mstat"), Buf("kr"), Buf("cossin"), Buf("gqk")
        mark = R.cur
        win = R.take(8 * 672 * 2, BF16).rearrange("p (k n) -> p k n", k=8)
        winb = Buf("win")
        glat = R.take(640 * 4)
        glb = Buf("glat")
        grow = R.take(640 * 4)
        growb = Buf("grow")
        lat = [R.take(640 * 2, BF16) for _ in range(2)]
        latb = [Buf("lat0"), Buf("lat1")]
        junk = R.take(384 * 4)
        jb = Buf("junk")
        pi = R.take(512, I32)
        pf = R.take(512)
        posT = R.take(64)
        tq = R.take(1024).rearrange("p (t n) -> p t n", t=NT)
        ti = R.take(1024, I32).rearrange("p (t n) -> p t n", t=NT)
        tf = R.take(1024).rearrange("p (t n) -> p t n", t=NT)
        invf = R.take(64)
        pib, rb_ = Buf("pi"), Buf("ropetmp")
        Win = self.mla_w_in.rearrange("(k p) n -> p k n", p=128)
        winf = R.take(8 * 672 * 4).rearrange("p (k n) -> p k n", k=8)
        winfb = Buf("winf")
        for k in range(8):
            self.dma(SP, winf[:, k, :], Win[:, k, :], [], [winfb], winfb)
        self.cp(POOL, win[:, 0:4, :], winf[:, 0:4, :], [winfb], [winb])
        self.cp(DVE, win[:, 4:8, :], winf[:, 4:8, :], [winfb], [winb])
        self.dma(SP, grow[0:1, :], self.mla_g_lat[0:1, :], [], [growb], growb)
        self.dma(SP, pi[0:16, :], self.pos[:, :], [], [pib], pib)
        self.dma(SP, invf, self.invf[:, :], [], [rb_], rb_)
        for (a, b) in ((0, 512), (512, 640)):
            self.mm(self.ps[6][:, 0:b - a], self.ones[0:1, 0:128], grow[0:1, a:b], True, True,
                    [growb, self.cb], [self.pb[6]])
            self.cp(DVE, glat[:, a:b], self.ps[6][:, 0:b - a], [self.pb[6]], [glb])
        self.chk(41)
        self.dma(SP, grow[0:1, 0:96], self.mla_g_q[0:1, :], [glb], [growb], growb)
        self.dma(SP, grow[0:1, 128:224], self.mla_g_k[0:1, :], [glb], [growb], growb)
        self.stt(DVE, grow[0:1, 0:96], grow[0:1, 0:96], 96.0 ** -0.5, grow[0:1, 128:224], ALU.mult, ALU.mult,
                 [growb], [growb])
        self.row_to_cols(self.ps[6], self.pb[6], grow, growb, 96, m=96)
        self.cp(DVE, gqk[0:96, 0:1], self.ps[6][0:96, 0:1], [self.pb[6]], [gqb])
        self.chk(42)
        self.cp(DVE, pf[0:16, :], pi[0:16, :], [pib], [rb_])
        self.tr(self.ps[6][:, 0:16], pf[0:16, :], self.identf[0:16, 0:16], [rb_, self.cb], [self.pb[6]])
        self.cp(DVE, posT[:, 0:16], self.ps[6][:, 0:16], [self.pb[6]], [rb_])
        for t in range(NT):
            self.ts(DVE, tq[:, t, :], invf[:, 0:16], posT[:, t:t + 1], None, ALU.mult, None, [rb_], [rb_])
        for (dst, shift) in ((sin, 0.0), (cos, 0.25)):
            if shift != 0.0:
                self.ts(DVE, tq, tq, shift, None, ALU.add, None, [rb_], [rb_])
            self.cp(DVE, ti, tq, [rb_], [rb_])
            self.cp(DVE, tf, ti, [rb_], [rb_])
            self.tt(DVE, tf, tq, tf, ALU.subtract, [rb_], [rb_])
            self.act(dst, tf, AF.Sin, [rb_], [csb], scale=2 * math.pi * (1 - 1e-6))
        self.chk(43)
        self.memset(DVE, ssq, 0.0, [stb])
        self.memset(DVE, sskv, 0.0, [stb])
        self.memset(DVE, rkv, 0.0, [stb])
        tpl = [self.psb[2], self.psb[3]]
        for t in range(NT):
            s = t % 2
            b0, b1 = (0, 1) if s == 0 else (4, 5)
            for k in range(8):
                self.mm(self.ps[b0][:, :], self.HT[:, k, t * 128:(t + 1) * 128], win[:, k, 0:512], k == 0, k == 7,
                        [self.HTb, winb], [self.pb[b0]])
            for k in range(8):
                self.mm(self.ps[b1][:, 0:160], self.HT[:, k, t * 128:(t + 1) * 128], win[:, k, 512:672], k == 0,
                        k == 7, [self.HTb, winb], [self.pb[b1]])
            if "Q" in self.dbg:
                continue
            self.act(junk[:, 0:384], self.ps[b0][:, 0:384], AF.Square, [self.pb[b0]], [jb, stb], accum=ssq[:, t:t + 1])
            self.tt(DVE, lat[s][:, 0:512], self.ps[b0][:, :], glat[:, 0:512], ALU.mult, [self.pb[b0], glb], [latb[s]])
            self.act(junk[:, 0:128], self.ps[b0][:, 384:512], AF.Square, [self.pb[b0]], [jb, stb],
                     accum=sskv[:, t:t + 1])
            self.act(junk[:, 128:256], self.ps[b1][:, 0:128], AF.Square, [self.pb[b1]], [jb, stb],
                     accum=rkv[:, t:t + 1])
            self.tt(DVE, lat[s][:, 512:640], self.ps[b1][:, 0:128], glat[:, 512:640], ALU.mult,
                    [self.pb[b1], glb], [latb[s]])
            self.cp(DVE, kr[:, t, :], self.ps[b1][:, 128:160], [self.pb[b1]], [krb])
            if "T" in self.dbg:
                continue
            for j in range(5):
                self.tr(tpl[s][:, j * 128:(j + 1) * 128], lat[s][:, j * 128:(j + 1) * 128], self.identb,
                        [latb[s], self.cb], [self.pb[2 + s]])
            if "C" in self.dbg:
                continue
            self.act(latt[:, :, t * 128:(t + 1) * 128],
                     tpl[s][:, 0:640].rearrange("p (j n) -> p j n", j=5), AF.Copy, [self.pb[2 + s]], [lattb])
        self.chk(44)
        self.tt(DVE, sskv, sskv, rkv, ALU.add, [stb], [stb])
        self.act(rkv, sskv, AF.Sqrt, [stb, self.cb], [stb], scale=1.0 / 256, bias=self.epsc[:, 0:1])
        self.recip(rkv, rkv, [stb], [stb])
        self.ts(DVE, epsq, ssq, EPS / 384, EPS * EPS, ALU.mult, ALU.add, [stb], [stb])
        tmp = tq
        tmp2 = tf
        x1, x2 = kr[:, :, 0:16], kr[:, :, 16:32]
        self.tt(DVE, tmp, x1, cos, ALU.mult, [krb, csb], [rb_])
        self.tt(DVE, tmp2, x2, sin, ALU.mult, [krb, csb], [rb_])
        self.tt(DVE, krr[:, :, 0:16], tmp, tmp2, ALU.subtract, [rb_], [krb])
        self.tt(DVE, tmp, x2, cos, ALU.mult, [krb, csb], [rb_])
        self.tt(DVE, tmp2, x1, sin, ALU.mult, [krb, csb], [rb_])
        self.tt(DVE, krr[:, :, 16:32], tmp, tmp2, ALU.add, [rb_], [krb])
        P.barrier()
        self.chk(4)

        R.cur = mark
        wq = R.take(3 * 1536 * 2, BF16).rearrange("p (k n) -> p k n", k=3)
        wkv = R.take(2 * 2048 * 2, BF16).rearrange("p (k n) -> p k n", k=2)
        wqb, wkvb = Buf("wq"), Buf("wkv")
        Wq = self.mla_w_qb.rearrange("(k p) n -> p k n", p=128)
        Wkv = self.mla_w_kvb.rearrange("(k p) n -> p k n", p=128)
        for k in range(3):
            self.dma(POOL, wq[:, k, :], Wq[:, k, :], [], [wqb], wqb)
        for k in range(2):
            self.dma(POOL, wkv[:, k, :], Wkv[:, k, :], [], [wkvb], wkvb)
        va = R.take(16 * 4 * 65 * 2, BF16).rearrange("p (t h n) -> p t h n", t=NT, h=4)
        vab = Buf("va")
        ont = R.take(4 * 2048 * 2, BF16).rearrange("p (h n) -> p h n", h=4)
        ontb = Buf("ont")
        wo = R.take(4 * 1024 * 2, BF16).rearrange("p (h n) -> p h n", h=4)
        wob = Buf("wo")
        qs = R.take(384 * 4).rearrange("p (h n) -> p h n", h=4)
        sq = R.take(384 * 4).rearrange("p (h n) -> p h n", h=4)
        kvs = R.take(512 * 4).rearrange("p (h n) -> p h n", h=4)
        kf = R.take(384 * 4).rearrange("p (h n) -> p h n", h=4)
        qn = R.take(384 * 2, BF16).rearrange("p (h n) -> p h n", h=4)
        kn = R.take(384 * 2, BF16).rearrange("p (h n) -> p h n", h=4)
        rt = R.take(4 * 4 * 16 * 4).rearrange("p (a h n) -> p a h n", a=4, h=4)
        s4 = R.take(64)
        qsb, sqb, kvsb, kfb, qnb, knb, rtb, s4b = (Buf(n) for n in ("qs", "sq", "kvs", "kf", "qn", "kn", "rt", "s4"))
        pt = [R.take(512 * 2, BF16) for _ in range(3)]
        ptb = [Buf("pt%d" % i) for i in range(3)]
        osb_ = R.take(512 * 4)
        osb = Buf("os")
        rrow = R.take(512 * 4)
        rrb = Buf("rrow")
        ytmp = [R.take(4096) for _ in range(2)]
        ytb = [Buf("yt0"), Buf("yt1")]
        QT = self.view(64 * 1024, 16 * 1024, BF16).rearrange("p (h n) -> p h n", h=4)
        KT = self.view(80 * 1024, 16 * 1024, BF16).rearrange("p (h n) -> p h n", h=4)
        qtb, ktb = Buf("QT"), Buf("KT")
        self.memset(POOL, va[:, :, :, 64:65], 1.0, [vab])
        tpq = self.psb[2].rearrange("p (h n) -> p h n", h=8)
        Wo = self.mla_w_o
        pcnt = 0
        for hg in range(4):
            self.dma(POOL, wo[0:64, :, :], Wo[hg * 256:(hg + 1) * 256, :].rearrange("(h d) n -> d h n", d=64),
                     [], [wob], wob)
            for t in range(NT):
                tsl = slice(t * 128, (t + 1) * 128)
                for k in range(3):
                    self.mm(self.ps[0][:, 0:384], latt[:, k, tsl], wq[:, k, hg * 384:(hg + 1) * 384], k == 0, k == 2,
                            [lattb, wqb], [self.pb[0]])
                for k in range(2):
                    self.mm(self.ps[1][:, :], latt[:, 3 + k, tsl], wkv[:, k, hg * 512:(hg + 1) * 512], k == 0, k == 1,
                            [lattb, wkvb], [self.pb[1]])
                self.act(qs.rearrange("p h n -> p (h n)"), self.ps[0][:, 0:384], AF.Copy, [self.pb[0]], [qsb])
                cb_ = cos[:, t, :].unsqueeze(1).to_broadcast([128, 4, 16])
                sb_ = sin[:, t, :].unsqueeze(1).to_broadcast([128, 4, 16])
                x1, x2 = qs[:, :, 64:80], qs[:, :, 80:96]
                self.tt(DVE, rt[:, 0], x1, cb_, ALU.mult, [qsb, csb], [rtb])
                self.tt(DVE, rt[:, 1], x2, sb_, ALU.mult, [qsb, csb], [rtb])
                self.tt(DVE, rt[:, 2], x2, cb_, ALU.mult, [qsb, csb], [rtb])
                self.tt(DVE, rt[:, 3], x1, sb_, ALU.mult, [qsb, csb], [rtb])
                self.tt(DVE, x1, rt[:, 0], rt[:, 1], ALU.subtract, [rtb], [qsb])
                self.tt(DVE, x2, rt[:, 2], rt[:, 3], ALU.add, [rtb], [qsb])
                self.act(sq.rearrange("p h n -> p (h n)"), qs.rearrange("p h n -> p (h n)"), AF.Square, [qsb], [sqb])
                self.reduce(s4[:, 0:4], sq, [sqb], [s4b])
                self.act(s4[:, 0:4], s4[:, 0:4], AF.Sqrt, [s4b, stb], [s4b], scale=1.0 / 96, bias=epsq[:, t:t + 1])
                self.recip(s4[:, 0:4], s4[:, 0:4], [s4b], [s4b])
                self.tt(DVE, qn, qs, s4[:, 0:4].unsqueeze(2).to_broadcast([128, 4, 96]), ALU.mult, [qsb, s4b], [qnb])
                for h in range(4):
                    self.tr(tpq[0:96, h, :], qn[:, h, :], self.identb, [qnb, self.cb], [self.pb[2]])
                self.act(QT[0:96, :, tsl], tpq[0:96, 0:4, :], AF.Identity, [self.pb[2], gqb], [qtb, self.HTb],
                         scale=gqk[0:96, 0:1])
                self.act(kvs.rearrange("p h n -> p (h n)"), self.ps[1][:, :], AF.Copy, [self.pb[1]], [kvsb])
                self.ts(DVE, kf[:, :, 0:64], kvs[:, :, 0:64], rkv[:, t:t + 1], None, ALU.mult, None,
                        [kvsb, stb], [kfb])
                self.cp(DVE, kf[:, :, 64:96], krr[:, t, :].unsqueeze(1).to_broadcast([128, 4, 32]), [krb], [kfb])
                self.act(sq.rearrange("p h n -> p (h n)"), kf.rearrange("p h n -> p (h n)"), AF.Square, [kfb], [sqb])
                self.reduce(s4[:, 4:8], sq, [sqb], [s4b])
                self.act(s4[:, 4:8], s4[:, 4:8], AF.Sqrt, [s4b, self.cb], [s4b], scale=1.0 / 96, bias=self.epsc[:, 0:1])
                self.recip(s4[:, 4:8], s4[:, 4:8], [s4b], [s4b])
                self.tt(DVE, kn, kf, s4[:, 4:8].unsqueeze(2).to_broadcast([128, 4, 96]), ALU.mult, [kfb, s4b], [knb])
                for h in range(4):
                    self.tr(tpq[0:96, 4 + h, :], kn[:, h, :], self.identb, [knb, self.cb], [self.pb[2]])
                self.act(KT[0:96, :, tsl], tpq[0:96, 4:8, :], AF.Copy, [self.pb[2]], [ktb, self.HTb])
                self.act(va[:, t, :, 0:64], kvs[:, :, 64:128], AF.Identity, [kvsb, stb], [vab], scale=rkv[:, t:t + 1])
            self.chk(5)
            for h in range(4):
                for qg in range(4):
                    qsl = slice(qg * 512, (qg + 1) * 512)
                    for kt in range(NT):
                        sbk = 3 + pcnt % 2
                        ps_ = pcnt % 3
                        pcnt += 1
                        self.mm(self.ps[sbk][:, :], KT[0:96, h, kt * 128:(kt + 1) * 128], QT[0:96, h, qsl], True, True,
                                [ktb, qtb], [self.pb[sbk]])
                        self.act(pt[ps_], self.ps[sbk][:, :], AF.Exp, [self.pb[sbk]], [ptb[ps_]])
                        self.mm(self.ps[5][0:65, :], va[:, kt, h, :], pt[ps_], kt == 0, kt == NT - 1,
                                [vab, ptb[ps_]], [self.pb[5]])
                    self.act(osb_[0:65, :], self.ps[5][0:65, :], AF.Copy, [self.pb[5]], [osb])
                    self.recip(rrow[64:65, 0:512], osb_[64:65, :], [osb], [rrb])
                    self.mm(self.ps[7][0:64, :], self.ones[64:65, 0:64], rrow[64:65, 0:512], True, True,
                            [rrb, self.cb], [self.pb[7]])
                    self.tt(DVE, ont[0:64, h, qsl], osb_[0:64, :], self.ps[7][0:64, :], ALU.mult,
                            [osb, self.pb[7]], [ontb])
            self.out_proj(ont, ontb, [], wo, wob, ytmp, ytb)

    def dil_layer(self):
        R = self.R
        R.reset()
        P = self.P
        tab = R.take(48 * 4)
        ohs = R.take(1152 * 4)
        tbx = R.take(48 * 128 * 4).rearrange("p (g n) -> p g n", g=48)
        rbs = [R.take(384 * 2, BF16) for _ in range(2)]
        tabb, ohb, tbxb = Buf("tab"), Buf("oh"), Buf("tbx")
        rbb = [Buf("rb0"), Buf("rb1")]
        zb = Buf("Z")
        gcb = Buf("gcols")
        self.memset(DVE, tab[32:64, 0:48], -30000.0, [tabb])
        self.dma(SP, tab[0:32, 0:48], self.rel_bias[:, :], [], [tabb], tabb)
        self.dma(SP, ohs[0:33, :], self.oh[:, :], [], [ohb], ohb)
        self.cp(DVE, tbx[0:33], tab[0:33, 0:48].unsqueeze(2).to_broadcast([33, 48, 128]), [tabb], [tbxb])
        for gh in range(48):
            g = gh // 16
            s = gh % 2
            self.mm(self.ps[s][:, 0:384], tbx[0:33, gh, :], ohs[0:33, g * 384:(g + 1) * 384], True, True,
                    [tbxb, ohb], [self.pb[s]])
            self.act(rbs[s], self.ps[s][:, 0:384], AF.Exp, [self.pb[s]], [rbb[s]])
            zw = bass.AP(self.zscr, gh * 128 * 640, [[641, 128], [1, 384]])
            self.dma(SP, zw, rbs[s], [rbb[s]], [zb], zb)
        P.barrier()

        R.reset()
        gcols2 = R.take(32)
        et = R.take(12 * 256 * 2, BF16).rearrange("p (g n) -> p g n", g=12)
        etb = Buf("et")
        wd = R.take(8 * 768 * 2, BF16).rearrange("p (k n) -> p k n", k=8)
        wdb = Buf("wd")
        qk_off = R.cur
        QT = R.take(2 * 2048 * 2, BF16).rearrange("p (a n) -> p a n", a=2)
        KT = R.take(2 * 2048 * 2, BF16).rearrange("p (a n) -> p a n", a=2)
        qtb, ktb = Buf("QT1"), Buf("KT1")
        ontv = self.view(qk_off, 16384, BF16).rearrange("p (h n) -> p h n", h=4)
        va = R.take(16 * 4 * 65 * 2, BF16).rearrange("p (t h n) -> p t h n", t=NT, h=4)
        vab = Buf("va1")
        acc = R.take(4 * 2048 * 4).rearrange("p (h n) -> p h n", h=4)
        accb = Buf("acc")
        wo = R.take(4 * 1024 * 2, BF16).rearrange("p (h n) -> p h n", h=4)
        wob = Buf("wo1")
        qs = R.take(512 * 4)
        sq = R.take(512 * 4)
        qkn = R.take(512 * 2, BF16)
        s8 = R.take(32)
        qsb, sqb, qknb, s8b = Buf("qs1"), Buf("sq1"), Buf("qkn"), Buf("s8")
        pt = [R.take(256 * 2, BF16) for _ in range(3)]
        ptb = [Buf("pt1_%d" % i) for i in range(3)]
        rrow = R.take(512 * 4)
        rrb = Buf("rrow1")
        ytmp = [R.take(4096) for _ in range(2)]
        ytb = [Buf("yt1_0"), Buf("yt1_1")]
        ontb = Buf("ont1")
        grow2 = acc[:, 0, :]
        g42 = grow2[:, 0:768].rearrange("p (a b n) -> p a b n", a=6, b=2)
        for rep in range(2):
            self.dma(SP, g42[0:1, 0:3, rep, :], self.dil_g_q[0:1, :].rearrange("o (g n) -> o g n", g=3), [], [accb], accb)
            self.dma(SP, g42[0:1, 3:6, rep, :], self.dil_g_k[0:1, :].rearrange("o (g n) -> o g n", g=3), [], [accb], accb)
        self.ts(DVE, grow2[0:1, 0:384], grow2[0:1, 0:384], 0.125, None, ALU.mult, None, [accb], [accb])
        self.row_to_cols(self.ps[6], self.pb[6], grow2, accb, 768)
        self.cp(DVE, gcols2[:, 0:6], self.ps[6][:, 0:6], [self.pb[6]], [gcb])
        self.memset(POOL, va[:, :, :, 64:65], 1.0, [vab])
        Wd = self.dil_w_in.rearrange("(k p) n -> p k n", p=128)
        tpq = self.psb[2].rearrange("p (a n) -> p a n", a=8)
        wins = [_windows(d) for (_, d) in DIL]
        pcnt = 0
        ocnt = 0
        for hb in range(4):
            self.dma(POOL, wo[0:64, :, :], self.dil_w_o[hb * 256:(hb + 1) * 256, :].rearrange("(h d) n -> d h n", d=64),
                     [], [wob], wob)
            for g in range(3):
                for hh in range(4):
                    gh = g * 16 + hb * 4 + hh
                    zr = bass.AP(self.zscr, gh * 128 * 640 + 127, [[640, 128], [1, 256]])
                    self.dma(SP, et[:, g * 4 + hh, :], zr, [zb], [etb], etb)
            for g, (_, d) in enumerate(DIL):
                L = S // d
                for part in range(3):
                    c0 = g * 3072 + part * 1024 + hb * 256
                    for k in range(8):
                        self.dma(POOL, wd[:, k, part * 256:(part + 1) * 256], Wd[:, k, c0:c0 + 256], [], [wdb], wdb)
                for ct in range(NT):
                    r = (128 * ct) // L
                    j0 = (128 * ct) % L
                    tok0 = r + d * j0
                    csl = slice(ct * 128, (ct + 1) * 128)
                    for k in range(8):
                        lhs = self.HT[:, k, tok0:tok0 + 127 * d + 1:d] if d > 1 else self.HT[:, k, tok0:tok0 + 128]
                        self.mm(self.ps[0][:, :], lhs, wd[:, k, 0:512], k == 0, k == 7, [self.HTb, wdb], [self.pb[0]])
                    for k in range(8):
                        lhs = self.HT[:, k, tok0:tok0 + 127 * d + 1:d] if d > 1 else self.HT[:, k, tok0:tok0 + 128]
                        self.mm(self.ps[1][:, 0:256], lhs, wd[:, k, 512:768], k == 0, k == 7, [self.HTb, wdb],
                                [self.pb[1]])
                    self.act(qs, self.ps[0][:, :], AF.Copy, [self.pb[0]], [qsb])
                    self.act(sq, qs, AF.Square, [qsb], [sqb])
                    self.reduce(s8[:, 0:8], sq.rearrange("p (h n) -> p h n", h=8), [sqb], [s8b])
                    self.act(s8[:, 0:8], s8[:, 0:8], AF.Sqrt, [s8b, self.cb], [s8b], scale=1.0 / 64, bias=self.epsc[:, 0:1])
                    self.recip(s8[:, 0:8], s8[:, 0:8], [s8b], [s8b])
                    self.tt(DVE, qkn.rearrange("p (h n) -> p h n", h=8), qs.rearrange("p (h n) -> p h n", h=8),
                            s8[:, 0:8].unsqueeze(2).to_broadcast([128, 8, 64]), ALU.mult, [qsb, s8b], [qknb])
                    for a in range(4):
                        self.tr(tpq[:, a, :], qkn[:, a * 128:(a + 1) * 128], self.identb, [qknb, self.cb], [self.pb[2]])
                    self.act(QT[:, :, csl], tpq[:, 0:2, :], AF.Identity, [self.pb[2], gcb], [qtb, ontb],
                             scale=gcols2[:, g:g + 1])
                    self.act(KT[:, :, csl], tpq[:, 2:4, :], AF.Identity, [self.pb[2], gcb], [ktb, ontb],
                             scale=gcols2[:, 3 + g:4 + g])
                    self.cp(DVE, va[:, ct, :, 0:64], self.ps[1][:, 0:256].rearrange("p (h n) -> p h n", h=4),
                            [self.pb[1]], [vab])
                for hh in range(4):
                    pr, hf = hh // 2, hh % 2
                    psl = slice(64 * hf, 64 * hf + 64)
                    accv = acc[0:65, hh, :]
                    for bank in range(4):
                        ob = 5 + ocnt % 2
                        ocnt += 1
                        ents = wins[g][bank]
                        for ei, (kt, qa, qb, eoff) in enumerate(ents):
                            n = qb - qa
                            sbk = 3 + pcnt % 2
                            ps_ = pcnt % 3
                            pcnt += 1
                            self.mm(self.ps[sbk][:, 0:n], KT[psl, pr, kt * 128:(kt + 1) * 128], QT[psl, pr, qa:qb],
                                    True, True, [ktb, qtb], [self.pb[sbk]])
                            self.act(pt[ps_][:, 0:n], self.ps[sbk][:, 0:n], AF.Exp, [self.pb[sbk]], [ptb[ps_]])
                            self.tt(DVE, pt[ps_][:, 0:n], pt[ps_][:, 0:n], et[:, g * 4 + hh, eoff:eoff + n], ALU.mult,
                                    [ptb[ps_], etb], [ptb[ps_]])
                            self.mm(self.ps[ob][0:65, qa - bank * 512:qb - bank * 512], va[:, kt, hh, :], pt[ps_][:, 0:n],
                                    ei == 0, ei == len(ents) - 1, [vab, ptb[ps_]], [self.pb[ob]], skip=True)
                        if d == 1:
                            dst = accv[:, bank * 512:(bank + 1) * 512]
                            src = self.ps[ob][0:65, :]
                        elif d == 4:
                            dst = accv[:, bank:2048:4]
                            src = self.ps[ob][0:65, :]
                        else:
                            dst = accv.rearrange("p (j r) -> p r j", r=16)[:, 4 * bank:4 * bank + 4, :]
                            src = self.ps[ob][0:65, :].rearrange("p (r j) -> p r j", r=4)
                        if g == 0:
                            self.act(dst, src, AF.Copy, [self.pb[ob]], [accb])
                        else:
                            self.tt(DVE, dst, src, dst, ALU.add, [self.pb[ob], accb], [accb])
            self.normalize_heads(lambda h, ch: (acc[0:64, h, ch * 512:(ch + 1) * 512],
                                                acc[64:65, h, ch * 512:(ch + 1) * 512]),
                                 [accb], ontv, ontb, [qtb, ktb], rrow, rrb)
            self.out_proj(ontv, ontb, [qtb, ktb], wo, wob, ytmp, ytb)


_CACHE = {}


def _program(layers):
    key = tuple(layers)
    if key not in _CACHE:
        _CACHE[key] = Builder(layers).build()
    return _CACHE[key]


def _in_maps(inp, x_override=None):
    f = lambda a: np.ascontiguousarray(np.asarray(a, dtype=np.float32))
    x = f(inp["x"]) if x_override is None else x_override
    pos = np.ascontiguousarray(np.asarray(inp["positions"], dtype=np.int32))
    shared = {
        "ada_w": f(inp["ada_w"]), "ada_b": f(inp["ada_b"]),
        "norm_mix": f(inp["norm_mix"]), "norm_mlp": f(inp["norm_mlp"]),
        "mlp_w1": f(inp["mlp_w1"]), "mlp_w2": f(inp["mlp_w2"]),
        "mla_w_in": f(inp["mla_w_in"])[0],
        "mla_g_lat": np.ascontiguousarray(np.concatenate([f(inp["mla_g_qa"])[0], f(inp["mla_g_kva"])[0]])[None, :]),
        "mla_w_qb": f(inp["mla_w_qb"])[0], "mla_w_kvb": f(inp["mla_w_kvb"])[0],
        "mla_g_q": f(inp["mla_g_q"]), "mla_g_k": f(inp["mla_g_k"]),
        "mla_w_o": f(inp["mla_w_o"])[0],
        "dil_w_in": f(inp["dil_w_in"])[0],
        "dil_g_q": f(inp["dil_g_q"]).reshape(1, 192), "dil_g_k": f(inp["dil_g_k"]).reshape(1, 192),
        "dil_w_o": f(inp["dil_w_o"])[0],
        "rel_bias": f(inp["rel_bias"]),
        "oh": _onehot_const(), "invf": _invf_const(),
    }
    maps = []
    for b in range(x.shape[0]):
        m = dict(shared)
        m["x"] = np.ascontiguousarray(x[b])
        m["c"] = np.ascontiguousarray(f(inp["c"])[b:b + 1])
        m["pos"] = np.ascontiguousarray(pos[b].reshape(16, 128))
        maps.append(m)
    return maps


LAUNCHES = ((0, 1),)


def kernel(**inputs):
    x = None
    for layers in LAUNCHES:
        nc = _program(layers)
        maps = _in_maps(inputs, x)
        res = run_bass_kernel_spmd(nc, maps, core_ids=list(range(len(maps))))
        x = np.stack([np.asarray(r["out"], dtype=np.float32) for r in res.results], axis=0)
    return x
```

```python
import contextlib
import math
import numpy as np
import concourse.bass as bass
import concourse.mybir as mybir
from concourse.bass_utils import run_bass_kernel_spmd

F32 = mybir.dt.float32
BF16 = mybir.dt.bfloat16
I32 = mybir.dt.int32
AF = mybir.ActivationFunctionType
ALU = mybir.AluOpType
AX = mybir.AxisListType

PE, ACT, DVE, POOL, SP = "pe", "act", "dve", "pool", "sp"
ENGS = [PE, ACT, DVE, POOL, SP]

S = 2048
D = 1024
NT = 16
EPS = 1e-6
DIL = ((128, 1), (512, 4), (2048, 16))


class Buf:
    __slots__ = ("name", "last_write", "readers", "dma_sem", "dma_cnt", "last_dma", "excl")

    _n = [0]

    def __init__(self, name, excl=False):
        self.excl = excl
        Buf._n[0] += 1
        self.name = "%s_%d" % (name, Buf._n[0])
        self.last_write = None
        self.readers = []
        self.dma_sem = None
        self.dma_cnt = 0
        self.last_dma = None


class Op:
    __slots__ = ("eng", "fn", "deps", "is_dma", "needs_inc", "ordinal", "dma_buf", "dma_val")

    def __init__(self, eng, fn, is_dma):
        self.eng = eng
        self.fn = fn
        self.deps = []
        self.is_dma = is_dma
        self.needs_inc = False
        self.ordinal = None
        self.dma_buf = None
        self.dma_val = None


class Prog:
    def __init__(self, nc):
        self.nc = nc
        self.ops = {e: [] for e in ENGS}
        self.dma_bufs = []
        self.fence = {e: [] for e in ENGS}

    def _add_dep(self, op, prod):
        if prod is None or prod is op:
            return
        if (not prod.is_dma) and prod.eng == op.eng and prod.eng == PE:
            return
        op.deps.append(prod)
        if not prod.is_dma:
            prod.needs_inc = True

    def op(self, eng, fn, reads=(), writes=(), dma_dst=None):
        o = Op(eng, fn, dma_dst is not None)
        if self.fence[eng]:
            for p in self.fence[eng]:
                if p.is_dma or p.eng != eng:
                    self._add_dep(o, p)
            self.fence[eng] = []
        for b in reads:
            self._add_dep(o, b.last_write)
            if b.excl:
                for r in b.readers:
                    if r.eng != eng:
                        self._add_dep(o, r)
        for b in writes:
            self._add_dep(o, b.last_write)
            for r in b.readers:
                self._add_dep(o, r)
        if dma_dst is not None:
            if dma_dst.dma_sem is None:
                dma_dst.dma_sem = "pending"
                self.dma_bufs.append(dma_dst)
            dma_dst.dma_cnt += 1
            o.dma_buf = dma_dst
            o.dma_val = 16 * dma_dst.dma_cnt
            dma_dst.last_dma = o
        for b in reads:
            b.readers.append(o)
        for b in writes:
            b.last_write = o
            b.readers = []
        self.ops[eng].append(o)
        return o

    def barrier(self):
        f = []
        for e in ENGS:
            last = None
            for o in reversed(self.ops[e]):
                if not o.is_dma:
                    last = o
                    break
            if last is not None:
                f.append(last)
        for b in self.dma_bufs:
            if b.last_dma is not None:
                f.append(b.last_dma)
        self.fence = {e: list(f) for e in ENGS}

    def emit(self, final_waits=()):
        nc = self.nc
        with contextlib.ExitStack() as st:
            sems = {e: st.enter_context(nc.semaphore("s_" + e)) for e in ENGS}
            for b in self.dma_bufs:
                b.dma_sem = st.enter_context(nc.semaphore("d_" + b.name))
            for e in ENGS:
                n = 0
                for o in self.ops[e]:
                    if o.needs_inc and not o.is_dma:
                        n += 1
                        o.ordinal = n
            block = st.enter_context(nc.Block())
            prog = self

            def run(e, h):
                waited = {}
                for o in prog.ops[e]:
                    for p in o.deps:
                        if p.is_dma:
                            s, v = p.dma_buf.dma_sem, p.dma_val
                        else:
                            s, v = sems[p.eng], p.ordinal
                        key = id(s)
                        if waited.get(key, 0) >= v:
                            continue
                        waited[key] = v
                        h.wait_ge(s, v)
                    ins = o.fn(h)
                    if o.is_dma:
                        ins.then_inc(o.dma_buf.dma_sem, 16)
                    elif o.needs_inc:
                        ins.then_inc(sems[e], 1)
                if e == SP:
                    for b in final_waits:
                        h.wait_ge(b.dma_sem, 16 * b.dma_cnt)

            @block.tensor
            def _(h):
                run(PE, h)

            @block.scalar
            def _(h):
                run(ACT, h)

            @block.vector
            def _(h):
                run(DVE, h)

            @block.gpsimd
            def _(h):
                run(POOL, h)

            @block.sync
            def _(h):
                run(SP, h)


def _t5_bucket_np(rel):
    n = np.abs(rel)
    thr = [15, 27, 50, 91, 166, 305, 559]
    mag = np.minimum(n, 8) + sum((n >= t).astype(np.int64) for t in thr)
    return (rel > 0) * 16 + mag


def _onehot_const():
    oh = np.zeros((33, 3, 384), np.float32)
    for g, (_, d) in enumerate(DIL):
        for m in range(384):
            j = 191 - m
            if abs(j) <= 64:
                oh[_t5_bucket_np(np.array([j * d]))[0], g, m] = 1.0
            else:
                oh[32, g, m] = 1.0
    return oh.reshape(33, 3 * 384)


def _invf_const():
    half = 16
    inv = 1.0 / (10000.0 ** (np.arange(half, dtype=np.float64) / half))
    return np.ascontiguousarray(np.broadcast_to((inv / (2 * math.pi)).astype(np.float32), (128, 16)))


def _windows(d):
    L = S // d
    nkt = L // 128
    banks = [[] for _ in range(4)]
    for r in range(d):
        for jt in range(nkt):
            lo = max(0, 128 * jt - 64)
            hi = min(L, 128 * jt + 192)
            a = r * L + lo
            b = r * L + hi
            while a < b:
                bank = a // 512
                e = min(b, (bank + 1) * 512)
                q_in = a - r * L
                eoff = q_in - (128 * jt - 64)
                banks[bank].append((r * nkt + jt, a, e, eoff))
                a = e
    return banks


class _Stop(Exception):
    pass


class Builder:
    def __init__(self, layers, stop=None, dbg=""):
        self.layers = tuple(layers)
        self.stop = stop
        self.dbg = dbg
        nc = self.nc = bass.Bass("TRN2", target_bir_lowering=False)
        self.P = Prog(nc)
        dt = nc.dram_tensor
        inp = lambda name, shape, ty=F32: dt(name, shape, ty, kind="ExternalInput").ap()
        self.x = inp("x", [S, D])
        self.c = inp("c", [1, D])
        self.pos = inp("pos", [16, 128], I32)
        self.ada_w = inp("ada_w", [2, D, 6 * D])
        self.ada_b = inp("ada_b", [2, 6 * D])
        self.norm_mix = inp("norm_mix", [2, D])
        self.norm_mlp = inp("norm_mlp", [2, D])
        self.w1 = inp("mlp_w1", [2, D, 4 * D])
        self.w2 = inp("mlp_w2", [2, 4 * D, D])
        self.mla_w_in = inp("mla_w_in", [D, 672])
        self.mla_g_lat = inp("mla_g_lat", [1, 640])
        self.mla_w_qb = inp("mla_w_qb", [384, 1536])
        self.mla_w_kvb = inp("mla_w_kvb", [256, 2048])
        self.mla_g_q = inp("mla_g_q", [1, 96])
        self.mla_g_k = inp("mla_g_k", [1, 96])
        self.mla_w_o = inp("mla_w_o", [D, D])
        self.dil_w_in = inp("dil_w_in", [D, 9216])
        self.dil_g_q = inp("dil_g_q", [1, 192])
        self.dil_g_k = inp("dil_g_k", [1, 192])
        self.dil_w_o = inp("dil_w_o", [D, D])
        self.rel_bias = inp("rel_bias", [32, 48])
        self.oh = inp("oh", [33, 1152])
        self.invf = inp("invf", [128, 16])
        self.out = dt("out", [S, D], F32, kind="ExternalOutput").ap()
        self.zscr = dt("zscr", [48 * 128 * 640], BF16, kind="Internal")

    def view(self, off, nbytes, dtype=F32, parts=128):
        assert off % 4 == 0 and nbytes % 4 == 0 and off + nbytes <= self.ARENA, (off, nbytes)
        a = self.arena[0:parts, off // 4:(off + nbytes) // 4]
        return a if dtype == F32 else a.bitcast(dtype)

    class Region:
        def __init__(self, b, start, end):
            self.b, self.start, self.end, self.cur = b, start, end, start

        def take(self, nbytes, dtype=F32):
            nbytes = (nbytes + 31) // 32 * 32
            assert self.cur + nbytes <= self.end, ("region overflow", self.cur, nbytes, self.end)
            v = self.b.view(self.cur, nbytes, dtype)
            self.cur += nbytes
            return v

        def reset(self):
            self.cur = self.start

    def mm(self, out, lhsT, rhs, start, stop, reads, writes, skip=False):
        self.P.op(PE, lambda e: e.matmul(out, lhsT=lhsT, rhs=rhs, start=start, stop=stop,
                                         skip_group_check=skip), reads, writes)

    def tr(self, out, in_, ident, reads, writes):
        self.P.op(PE, lambda e: e.transpose(out=out, in_=in_, identity=ident), reads, writes)

    def act(self, out, in_, func, reads, writes, scale=None, bias=None, accum=None):
        kw = {}
        if scale is not None:
            kw["scale"] = scale
        if bias is not None:
            kw["bias"] = bias
        if accum is not None:
            kw["accum_out"] = accum
        self.P.op(ACT, lambda e: e.activation(out=out, in_=in_, func=func, **kw), reads, writes)

    def tt(self, eng, out, in0, in1, op, reads, writes):
        self.P.op(eng, lambda e: e.tensor_tensor(out=out, in0=in0, in1=in1, op=op), reads, writes)

    def ts(self, eng, out, in0, s1, s2, op0, op1, reads, writes):
        if s2 is None:
            self.P.op(eng, lambda e: e.tensor_scalar(out=out, in0=in0, scalar1=s1, scalar2=None, op0=op0),
                      reads, writes)
        else:
            self.P.op(eng, lambda e: e.tensor_scalar(out=out, in0=in0, scalar1=s1, scalar2=s2, op0=op0, op1=op1),
                      reads, writes)

    def stt(self, eng, out, in0, scalar, in1, op0, op1, reads, writes):
        self.P.op(eng, lambda e: e.scalar_tensor_tensor(out=out, in0=in0, scalar=scalar, in1=in1, op0=op0, op1=op1),
                  reads, writes)

    def cp(self, eng, out, in_, reads, writes):
        self.P.op(eng, lambda e: e.tensor_copy(out=out, in_=in_), reads, writes)

    def memset(self, eng, ap, val, writes):
        self.P.op(eng, lambda e: e.memset(ap, val), (), writes)

    def recip(self, out, in_, reads, writes):
        self.P.op(DVE, lambda e: e.reciprocal(out=out, in_=in_), reads, writes)

    def reduce(self, out, in_, reads, writes):
        self.P.op(DVE, lambda e: e.tensor_reduce(out=out, in_=in_, axis=AX.X, op=ALU.add), reads, writes)

    def dma(self, eng, out, in_, reads, writes, dst):
        self.P.op(eng, lambda e: e.dma_start(out=out, in_=in_), reads, writes, dma_dst=dst)

    def row_to_cols(self, ps_ap, ps_buf, row_ap, row_buf, n, m=128, col0=0):
        for j in range(n // m):
            self.mm(ps_ap[0:m, col0 + j:col0 + j + 1], row_ap[0:1, j * m:(j + 1) * m], self.one[0:1, 0:1],
                    True, True, [row_buf, self.cb], [ps_buf])

    def build(self):
        nc = self.nc
        self.ARENA = 212800
        with contextlib.ExitStack() as st:
            self.arena = st.enter_context(nc.sbuf_tensor("arena", [128, self.ARENA // 4], F32))
            self.ps = [st.enter_context(nc.psum_tensor("ps%d" % i, [128, 512], F32)) for i in range(8)]
            self.pb = [Buf("ps%d" % i, excl=True) for i in range(8)]
            self.psb = [self.ps[i][:, :].bitcast(BF16) for i in range(8)]
            K = 1024
            self.X = self.view(0, 64 * K).rearrange("p (t n) -> p t n", t=NT)
            self.Xb = [Buf("X%d" % t) for t in range(NT)]
            self.HTr = Builder.Region(self, 64 * K, 96 * K)
            self.HT = self.view(64 * K, 32 * K, BF16).rearrange("p (c n) -> p c n", c=8)
            self.HTb = Buf("HT")
            self.gates = self.view(96 * K, 8 * K).rearrange("p (g n) -> p g n", g=2)
            self.gb = Buf("gates")
            cr = Builder.Region(self, 104 * K, 107 * K)
            self.identb = cr.take(256, BF16)
            self.identf = cr.take(512)
            self.ones = cr.take(512)
            self.one = self.ones
            self.cols = cr.take(128)
            self.epsc = cr.take(32)
            self.ss = cr.take(64)
            self.rstd = cr.take(64)
            self.condT = cr.take(32, BF16)
            self.cb = Buf("consts")
            self.colsb = Buf("cols")
            self.statb = Buf("stat")
            self.R = Builder.Region(self, 107 * K, self.ARENA)
            self.NS = Builder.Region(self, self.ARENA - 8192, self.ARENA)
            self.outb = Buf("out")

            try:
                self.setup()
                self.chk(1)
                for li in self.layers:
                    self.ada(li)
                    self.P.barrier()
                    self.chk(2)
                    self.norm_to_ht(0)
                    self.chk(3)
                    if li == 0:
                        self.mla_layer()
                    else:
                        self.dil_layer()
                    self.P.barrier()
                    self.chk(6)
                    self.norm_to_ht(16)
                    self.mlp(li)
                    self.P.barrier()
            except _Stop:
                self.P.barrier()
            for t in range(NT):
                self.dma(SP, self.out[t * 128:(t + 1) * 128, :], self.X[:, t, :], [self.Xb[t]], [self.outb], self.outb)
            self.P.emit(final_waits=[self.outb])
        return nc

    def chk(self, stage):
        if self.stop is not None and self.stop == stage:
            raise _Stop()

    def setup(self):
        for t in range(NT):
            self.dma(SP, self.X[:, t, :], self.x[t * 128:(t + 1) * 128, :], [], [self.Xb[t]], self.Xb[t])
        cb = self.cb
        self.memset(POOL, self.identf, 0.0, [cb])
        self.P.op(POOL, lambda e: e.affine_select(out=self.identf, in_=self.identf, pattern=[[-1, 128]],
                                                  compare_op=ALU.not_equal, fill=1.0, base=0,
                                                  channel_multiplier=1), [cb], [cb])
        self.cp(POOL, self.identb, self.identf, [cb], [cb])
        self.memset(POOL, self.ones, 1.0, [cb])
        self.memset(POOL, self.epsc, EPS, [cb])
        self.R.reset()
        crow = self.R.take(4096)
        crb = Buf("crow")
        self.dma(SP, crow[0:1, :], self.c[0:1, :], [], [crb], crb)
        self.act(crow[0:1, :], crow[0:1, :], AF.Silu, [crb], [crb])
        self.row_to_cols(self.ps[7], self.pb[7], crow, crb, 1024)
        self.cp(DVE, self.condT[:, 0:8], self.ps[7][:, 0:8], [self.pb[7]], [cb])
        self.P.barrier()

    def ada(self, li):
        self.R.reset()
        self.HTr.reset()
        modrow = self.HTr.take(24576)
        nmrow = self.HTr.take(8192)
        adab = self.R.take(24576)
        wa = [self.R.take(8 * 1536 * 2, BF16).rearrange("p (k n) -> p k n", k=8) for _ in range(2)]
        wab = [Buf("wa0"), Buf("wa1")]
        mb, nb, ab = Buf("modrow"), Buf("nmrow"), Buf("adab")
        self.dma(SP, adab[0:1, :], self.ada_b[li:li + 1, :], [], [ab], ab)
        self.dma(SP, nmrow[0:1, 0:1024], self.norm_mix[li:li + 1, :], [], [nb], nb)
        self.dma(SP, nmrow[0:1, 1024:2048], self.norm_mlp[li:li + 1, :], [], [nb], nb)
        aw = self.ada_w[li].rearrange("(k p) n -> p k n", p=128)
        for blk in range(4):
            s = blk % 2
            for k in range(8):
                self.dma(POOL, wa[s][:, k, :], aw[:, k, blk * 1536:(blk + 1) * 1536], [], [wab[s]], wab[s])
            for cg in range(3):
                pbank = 5 + (blk * 3 + cg) % 2
                for k in range(8):
                    self.mm(self.ps[pbank][0:1, :], self.condT[:, k:k + 1], wa[s][:, k, cg * 512:(cg + 1) * 512],
                            k == 0, k == 7, [wab[s], self.cb], [self.pb[pbank]])
                c0 = blk * 1536 + cg * 512
                self.tt(DVE, modrow[0:1, c0:c0 + 512], self.ps[pbank][0:1, :], adab[0:1, c0:c0 + 512], ALU.add,
                        [self.pb[pbank], ab], [mb])
        sl = lambda i: modrow[0:1, i * 1024:(i + 1) * 1024]
        self.stt(DVE, sl(1), sl(1), 1.0, nmrow[0:1, 0:1024], ALU.add, ALU.mult, [mb, nb], [mb])
        self.stt(DVE, sl(4), sl(4), 1.0, nmrow[0:1, 1024:2048], ALU.add, ALU.mult, [mb, nb], [mb])
        p7, pb7 = self.ps[7], self.pb[7]
        for j, src in enumerate((1, 0, 4, 3)):
            self.row_to_cols(p7, pb7, sl(src), mb, 1024, col0=8 * j)
        self.cp(DVE, self.cols[:, 0:32], p7[:, 0:32], [pb7], [self.colsb])
        for gi, src in enumerate((2, 5)):
            for hlf in range(2):
                pbank = 5 + hlf
                self.mm(self.ps[pbank][:, :], self.ones[0:1, 0:128], sl(src)[0:1, hlf * 512:(hlf + 1) * 512],
                        True, True, [mb, self.cb], [self.pb[pbank]])
                self.cp(DVE, self.gates[:, gi, hlf * 512:(hlf + 1) * 512], self.ps[pbank][:, :],
                        [self.pb[pbank]], [self.gb])

    def norm_to_ht(self, col0):
        self.NS.reset()
        junk = self.NS.take(4096)
        jb = Buf("junk")
        xn = [self.NS.take(2048, BF16) for _ in range(2)]
        xnb = [Buf("xn0"), Buf("xn1")]
        sb = self.statb
        self.memset(DVE, self.ss, 0.0, [sb])
        for t in range(NT):
            self.act(junk, self.X[:, t, :], AF.Square, [self.Xb[t]], [jb, sb], accum=self.ss[:, t:t + 1])
        self.act(self.rstd, self.ss, AF.Sqrt, [sb, self.cb], [sb], scale=1.0 / D, bias=self.epsc[:, 0:1])
        self.recip(self.rstd, self.rstd, [sb], [sb])
        tp = self.psb[2].rearrange("p (c n) -> p c n", c=8)
        tpb = self.pb[2]
        tp2 = self.psb[3].rearrange("p (c n) -> p c n", c=8)
        tps, tpbs = [tp, tp2], [self.pb[2], self.pb[3]]
        for t in range(NT):
            s = t % 2
            self.ts(DVE, xn[s], self.X[:, t, :], self.rstd[:, t:t + 1], None, ALU.mult, None,
                    [self.Xb[t], sb], [xnb[s]])
            for c in range(8):
                self.tr(tps[s][:, c, :], xn[s][:, c * 128:(c + 1) * 128], self.identb, [xnb[s], self.cb], [tpbs[s]])
            for c in range(8):
                o = self.HT[:, c, t * 128:(t + 1) * 128]
                sc = self.cols[:, col0 + c:col0 + c + 1]
                bi = self.cols[:, col0 + 8 + c:col0 + 8 + c + 1]
                if s == 0:
                    self.act(o, tps[s][:, c, :], AF.Identity, [tpbs[s], self.colsb], [self.HTb], scale=sc, bias=bi)
                else:
                    self.ts(DVE, o, tps[s][:, c, :], sc, bi, ALU.mult, ALU.add, [tpbs[s], self.colsb], [self.HTb])

    def resid_update(self, t, gi, banks, ytmp, ytb):
        for hlf in range(2):
            self.tt(DVE, ytmp[:, hlf * 512:(hlf + 1) * 512], self.ps[banks[hlf]][:, :],
                    self.gates[:, gi, hlf * 512:(hlf + 1) * 512], ALU.mult, [self.pb[banks[hlf]], self.gb], [ytb])
        self.tt(POOL, self.X[:, t, :], self.X[:, t, :], ytmp, ALU.add, [ytb, self.Xb[t]], [self.Xb[t]])

    def mlp(self, li):
        self.R.reset()
        R = self.R
        w1 = [R.take(8 * 1024 * 2, BF16).rearrange("p (k n) -> p k n", k=8) for _ in range(2)]
        w2 = [R.take(8 * 1024 * 2, BF16).rearrange("p (k n) -> p k n", k=8) for _ in range(2)]
        w1b = [Buf("w1_0"), Buf("w1_1")]
        w2b = [Buf("w2_0"), Buf("w2_1")]
        hid = [R.take(8 * 512 * 2, BF16).rearrange("p (k n) -> p k n", k=8) for _ in range(2)]
        hidb = [Buf("hid0"), Buf("hid1")]
        rs = [R.take(2048) for _ in range(2)]
        rsb = [Buf("rs0"), Buf("rs1")]
        ytmp = [R.take(4096) for _ in range(2)]
        ytb = [Buf("yt0"), Buf("yt1")]
        W1 = self.w1[li].rearrange("(k p) n -> p k n", p=128)
        W2 = self.w2[li].rearrange("(k p) n -> p k n", p=128)
        cnt = 0
        ycnt = 0
        for qh in range(4):
            s = qh % 2
            for k in range(8):
                self.dma(POOL, w1[s][:, k, :], W1[:, k, qh * 1024:(qh + 1) * 1024], [], [w1b[s]], w1b[s])
            for k in range(8):
                self.dma(POOL, w2[s][:, k, :], W2[:, qh * 8 + k, :], [], [w2b[s]], w2b[s])
            for tg in range(4):
                hs = tg % 2
                for hc in range(8):
                    pbk = cnt % 2
                    cnt += 1
                    for k in range(8):
                        self.mm(self.ps[pbk][:, :], w1[s][:, k, hc * 128:(hc + 1) * 128],
                                self.HT[:, k, tg * 512:(tg + 1) * 512], k == 0, k == 7,
                                [w1b[s], self.HTb], [self.pb[pbk]])
                    self.act(rs[pbk], self.ps[pbk][:, :], AF.Relu, [self.pb[pbk]], [rsb[pbk]])
                    self.tt(DVE, hid[hs][:, hc, :], rs[pbk], rs[pbk], ALU.mult, [rsb[pbk]], [hidb[hs]])
                for tt_ in range(4):
                    t = tg * 4 + tt_
                    ys = ycnt % 2
                    ycnt += 1
                    banks = (2 + 2 * ys, 3 + 2 * ys)
                    for hlf in range(2):
                        for hc in range(8):
                            self.mm(self.ps[banks[hlf]][:, :], hid[hs][:, hc, tt_ * 128:(tt_ + 1) * 128],
                                    w2[s][:, hc, hlf * 512:(hlf + 1) * 512], hc == 0, hc == 7,
                                    [hidb[hs], w2b[s]], [self.pb[banks[hlf]]])
                    self.resid_update(t, 1, banks, ytmp[ys], ytb[ys])

    def out_proj(self, ont, ontb, extra, wo, wob, ytmp, ytb):
        for t in range(NT):
            ys = t % 2
            banks = (0, 1) if ys == 0 else (2, 3)
            for hlf in range(2):
                for h in range(4):
                    self.mm(self.ps[banks[hlf]][:, :], ont[0:64, h, t * 128:(t + 1) * 128],
                            wo[0:64, h, hlf * 512:(hlf + 1) * 512], h == 0, h == 3,
                            [ontb, wob] + extra, [self.pb[banks[hlf]]])
            self.resid_update(t, 0, banks, ytmp[ys], ytb[ys])

    def normalize_heads(self, src_fn, src_bufs, ont, ontb, extra_w, rrow, rrb):
        for h in range(4):
            for ch in range(4):
                num, den = src_fn(h, ch)
                self.recip(rrow[64:65, 0:512], den, src_bufs, [rrb])
                self.mm(self.ps[7][0:64, :], self.ones[64:65, 0:64], rrow[64:65, 0:512], True, True,
                        [rrb, self.cb], [self.pb[7]])
                self.tt(DVE, ont[0:64, h, ch * 512:(ch + 1) * 512], num, self.ps[7][0:64, :], ALU.mult,
                        src_bufs + [self.pb[7]], [ontb] + extra_w)

    def mla_layer(self):
        R = self.R
        R.reset()
        P = self.P
        latt = R.take(5 * 2048 * 2, BF16).rearrange("p (j n) -> p j n", j=5)
        lattb = Buf("latt")
        epsq = R.take(64)
        rkv = R.take(64)
        ssq = R.take(64)
        sskv = R.take(64)
        kr = R.take(16 * 32 * 4).rearrange("p (t n) -> p t n", t=NT)
        krr = R.take(16 * 32 * 4).rearrange("p (t n) -> p t n", t=NT)
        cos = R.take(16 * 16 * 4).rearrange("p (t n) -> p t n", t=NT)
        sin = R.take(16 * 16 * 4).rearrange("p (t n) -> p t n", t=NT)
        gqk = R.take(32)
        stb, krb, csb, gqb = Buf("mstat"), Buf("kr"), Buf("cossin"), Buf("gqk")
        mark = R.cur
        win = R.take(8 * 672 * 2, BF16).rearrange("p (k n) -> p k n", k=8)
        winb = Buf("win")
        glat = R.take(640 * 4)
        glb = Buf("glat")
        grow = R.take(640 * 4)
        growb = Buf("grow")
        lat = [R.take(640 * 2, BF16) for _ in range(2)]
        latb = [Buf("lat0"), Buf("lat1")]
        junk = R.take(384 * 4)
        jb = Buf("junk")
        pi = R.take(512, I32)
        pf = R.take(512)
        posT = R.take(64)
        tq = R.take(1024).rearrange("p (t n) -> p t n", t=NT)
        ti = R.take(1024, I32).rearrange("p (t n) -> p t n", t=NT)
        tf = R.take(1024).rearrange("p (t n) -> p t n", t=NT)
        invf = R.take(64)
        pib, rb_ = Buf("pi"), Buf("ropetmp")
        Win = self.mla_w_in.rearrange("(k p) n -> p k n", p=128)
        winf = R.take(8 * 672 * 4).rearrange("p (k n) -> p k n", k=8)
        winfb = Buf("winf")
        for k in range(8):
            self.dma(SP, winf[:, k, :], Win[:, k, :], [], [winfb], winfb)
        self.cp(POOL, win[:, 0:4, :], winf[:, 0:4, :], [winfb], [winb])
        self.cp(DVE, win[:, 4:8, :], winf[:, 4:8, :], [winfb], [winb])
        self.dma(SP, grow[0:1, :], self.mla_g_lat[0:1, :], [], [growb], growb)
        self.dma(SP, pi[0:16, :], self.pos[:, :], [], [pib], pib)
        self.dma(SP, invf, self.invf[:, :], [], [rb_], rb_)
        for (a, b) in ((0, 512), (512, 640)):
            self.mm(self.ps[6][:, 0:b - a], self.ones[0:1, 0:128], grow[0:1, a:b], True, True,
                    [growb, self.cb], [self.pb[6]])
            self.cp(DVE, glat[:, a:b], self.ps[6][:, 0:b - a], [self.pb[6]], [glb])
        self.chk(41)
        self.dma(SP, grow[0:1, 0:96], self.mla_g_q[0:1, :], [glb], [growb], growb)
        self.dma(SP, grow[0:1, 128:224], self.mla_g_k[0:1, :], [glb], [growb], growb)
        self.stt(DVE, grow[0:1, 0:96], grow[0:1, 0:96], 96.0 ** -0.5, grow[0:1, 128:224], ALU.mult, ALU.mult,
                 [growb], [growb])
        self.row_to_cols(self.ps[6], self.pb[6], grow, growb, 96, m=96)
        self.cp(DVE, gqk[0:96, 0:1], self.ps[6][0:96, 0:1], [self.pb[6]], [gqb])
        self.chk(42)
        self.cp(DVE, pf[0:16, :], pi[0:16, :], [pib], [rb_])
        self.tr(self.ps[6][:, 0:16], pf[0:16, :], self.identf[0:16, 0:16], [rb_, self.cb], [self.pb[6]])
        self.cp(DVE, posT[:, 0:16], self.ps[6][:, 0:16], [self.pb[6]], [rb_])
        for t in range(NT):
            self.ts(DVE, tq[:, t, :], invf[:, 0:16], posT[:, t:t + 1], None, ALU.mult, None, [rb_], [rb_])
        for (dst, shift) in ((sin, 0.0), (cos, 0.25)):
            if shift != 0.0:
                self.ts(DVE, tq, tq, shift, None, ALU.add, None, [rb_], [rb_])
            self.cp(DVE, ti, tq, [rb_], [rb_])
            self.cp(DVE, tf, ti, [rb_], [rb_])
            self.tt(DVE, tf, tq, tf, ALU.subtract, [rb_], [rb_])
            self.act(dst, tf, AF.Sin, [rb_], [csb], scale=2 * math.pi * (1 - 1e-6))
        self.chk(43)
        self.memset(DVE, ssq, 0.0, [stb])
        self.memset(DVE, sskv, 0.0, [stb])
        self.memset(DVE, rkv, 0.0, [stb])
        tpl = [self.psb[2], self.psb[3]]
        for t in range(NT):
            s = t % 2
            b0, b1 = (0, 1) if s == 0 else (4, 5)
            for k in range(8):
                self.mm(self.ps[b0][:, :], self.HT[:, k, t * 128:(t + 1) * 128], win[:, k, 0:512], k == 0, k == 7,
                        [self.HTb, winb], [self.pb[b0]])
            for k in range(8):
                self.mm(self.ps[b1][:, 0:160], self.HT[:, k, t * 128:(t + 1) * 128], win[:, k, 512:672], k == 0,
                        k == 7, [self.HTb, winb], [self.pb[b1]])
            if "Q" in self.dbg:
                continue
            self.act(junk[:, 0:384], self.ps[b0][:, 0:384], AF.Square, [self.pb[b0]], [jb, stb], accum=ssq[:, t:t + 1])
            self.tt(DVE, lat[s][:, 0:512], self.ps[b0][:, :], glat[:, 0:512], ALU.mult, [self.pb[b0], glb], [latb[s]])
            self.act(junk[:, 0:128], self.ps[b0][:, 384:512], AF.Square, [self.pb[b0]], [jb, stb],
                     accum=sskv[:, t:t + 1])
            self.act(junk[:, 128:256], self.ps[b1][:, 0:128], AF.Square, [self.pb[b1]], [jb, stb],
                     accum=rkv[:, t:t + 1])
            self.tt(DVE, lat[s][:, 512:640], self.ps[b1][:, 0:128], glat[:, 512:640], ALU.mult,
                    [self.pb[b1], glb], [latb[s]])
            self.cp(DVE, kr[:, t, :], self.ps[b1][:, 128:160], [self.pb[b1]], [krb])
            if "T" in self.dbg:
                continue
            for j in range(5):
                self.tr(tpl[s][:, j * 128:(j + 1) * 128], lat[s][:, j * 128:(j + 1) * 128], self.identb,
                        [latb[s], self.cb], [self.pb[2 + s]])
            if "C" in self.dbg:
                continue
            self.act(latt[:, :, t * 128:(t + 1) * 128],
                     tpl[s][:, 0:640].rearrange("p (j n) -> p j n", j=5), AF.Copy, [self.pb[2 + s]], [lattb])
        self.chk(44)
        self.tt(DVE, sskv, sskv, rkv, ALU.add, [stb], [stb])
        self.act(rkv, sskv, AF.Sqrt, [stb, self.cb], [stb], scale=1.0 / 256, bias=self.epsc[:, 0:1])
        self.recip(rkv, rkv, [stb], [stb])
        self.ts(DVE, epsq, ssq, EPS / 384, EPS * EPS, ALU.mult, ALU.add, [stb], [stb])
        tmp = tq
        tmp2 = tf
        x1, x2 = kr[:, :, 0:16], kr[:, :, 16:32]
        self.tt(DVE, tmp, x1, cos, ALU.mult, [krb, csb], [rb_])
        self.tt(DVE, tmp2, x2, sin, ALU.mult, [krb, csb], [rb_])
        self.tt(DVE, krr[:, :, 0:16], tmp, tmp2, ALU.subtract, [rb_], [krb])
        self.tt(DVE, tmp, x2, cos, ALU.mult, [krb, csb], [rb_])
        self.tt(DVE, tmp2, x1, sin, ALU.mult, [krb, csb], [rb_])
        self.tt(DVE, krr[:, :, 16:32], tmp, tmp2, ALU.add, [rb_], [krb])
        P.barrier()
        self.chk(4)

        R.cur = mark
        wq = R.take(3 * 1536 * 2, BF16).rearrange("p (k n) -> p k n", k=3)
        wkv = R.take(2 * 2048 * 2, BF16).rearrange("p (k n) -> p k n", k=2)
        wqb, wkvb = Buf("wq"), Buf("wkv")
        Wq = self.mla_w_qb.rearrange("(k p) n -> p k n", p=128)
        Wkv = self.mla_w_kvb.rearrange("(k p) n -> p k n", p=128)
        for k in range(3):
            self.dma(POOL, wq[:, k, :], Wq[:, k, :], [], [wqb], wqb)
        for k in range(2):
            self.dma(POOL, wkv[:, k, :], Wkv[:, k, :], [], [wkvb], wkvb)
        va = R.take(16 * 4 * 65 * 2, BF16).rearrange("p (t h n) -> p t h n", t=NT, h=4)
        vab = Buf("va")
        ont = R.take(4 * 2048 * 2, BF16).rearrange("p (h n) -> p h n", h=4)
        ontb = Buf("ont")
        wo = R.take(4 * 1024 * 2, BF16).rearrange("p (h n) -> p h n", h=4)
        wob = Buf("wo")
        qs = R.take(384 * 4).rearrange("p (h n) -> p h n", h=4)
        sq = R.take(384 * 4).rearrange("p (h n) -> p h n", h=4)
        kvs = R.take(512 * 4).rearrange("p (h n) -> p h n", h=4)
        kf = R.take(384 * 4).rearrange("p (h n) -> p h n", h=4)
        qn = R.take(384 * 2, BF16).rearrange("p (h n) -> p h n", h=4)
        kn = R.take(384 * 2, BF16).rearrange("p (h n) -> p h n", h=4)
        rt = R.take(4 * 4 * 16 * 4).rearrange("p (a h n) -> p a h n", a=4, h=4)
        s4 = R.take(64)
        qsb, sqb, kvsb, kfb, qnb, knb, rtb, s4b = (Buf(n) for n in ("qs", "sq", "kvs", "kf", "qn", "kn", "rt", "s4"))
        pt = [R.take(512 * 2, BF16) for _ in range(4)]
        ptb = [Buf("pt%d" % i) for i in range(4)]
        osb_ = R.take(512 * 4)
        osb = Buf("os")
        rrow = R.take(512 * 4)
        rrb = Buf("rrow")
        ytmp = [R.take(4096) for _ in range(2)]
        ytb = [Buf("yt0"), Buf("yt1")]
        QT = self.view(64 * 1024, 16 * 1024, BF16).rearrange("p (h n) -> p h n", h=4)
        KT = self.view(80 * 1024, 16 * 1024, BF16).rearrange("p (h n) -> p h n", h=4)
        qtb, ktb = Buf("QT"), Buf("KT")
        self.memset(POOL, va[:, :, :, 64:65], 1.0, [vab])
        tpq = self.psb[2].rearrange("p (h n) -> p h n", h=8)
        Wo = self.mla_w_o
        pcnt = 0
        for hg in range(4):
            self.dma(POOL, wo[0:64, :, :], Wo[hg * 256:(hg + 1) * 256, :].rearrange("(h d) n -> d h n", d=64),
                     [], [wob], wob)
            for t in range(NT):
                tsl = slice(t * 128, (t + 1) * 128)
                for k in range(3):
                    self.mm(self.ps[0][:, 0:384], latt[:, k, tsl], wq[:, k, hg * 384:(hg + 1) * 384], k == 0, k == 2,
                            [lattb, wqb], [self.pb[0]])
                for k in range(2):
                    self.mm(self.ps[1][:, :], latt[:, 3 + k, tsl], wkv[:, k, hg * 512:(hg + 1) * 512], k == 0, k == 1,
                            [lattb, wkvb], [self.pb[1]])
                self.act(qs.rearrange("p h n -> p (h n)"), self.ps[0][:, 0:384], AF.Copy, [self.pb[0]], [qsb])
                cb_ = cos[:, t, :].unsqueeze(1).to_broadcast([128, 4, 16])
                sb_ = sin[:, t, :].unsqueeze(1).to_broadcast([128, 4, 16])
                x1, x2 = qs[:, :, 64:80], qs[:, :, 80:96]
                self.tt(DVE, rt[:, 0], x1, cb_, ALU.mult, [qsb, csb], [rtb])
                self.tt(DVE, rt[:, 1], x2, sb_, ALU.mult, [qsb, csb], [rtb])
                self.tt(DVE, rt[:, 2], x2, cb_, ALU.mult, [qsb, csb], [rtb])
                self.tt(DVE, rt[:, 3], x1, sb_, ALU.mult, [qsb, csb], [rtb])
                self.tt(DVE, x1, rt[:, 0], rt[:, 1], ALU.subtract, [rtb], [qsb])
                self.tt(DVE, x2, rt[:, 2], rt[:, 3], ALU.add, [rtb], [qsb])
                self.act(sq.rearrange("p h n -> p (h n)"), qs.rearrange("p h n -> p (h n)"), AF.Square, [qsb], [sqb])
                self.reduce(s4[:, 0:4], sq, [sqb], [s4b])
                self.act(s4[:, 0:4], s4[:, 0:4], AF.Sqrt, [s4b, stb], [s4b], scale=1.0 / 96, bias=epsq[:, t:t + 1])
                self.recip(s4[:, 0:4], s4[:, 0:4], [s4b], [s4b])
                self.tt(DVE, qn, qs, s4[:, 0:4].unsqueeze(2).to_broadcast([128, 4, 96]), ALU.mult, [qsb, s4b], [qnb])
                for h in range(4):
                    self.tr(tpq[0:96, h, :], qn[:, h, :], self.identb, [qnb, self.cb], [self.pb[2]])
                self.act(QT[0:96, :, tsl], tpq[0:96, 0:4, :], AF.Identity, [self.pb[2], gqb], [qtb, self.HTb],
                         scale=gqk[0:96, 0:1])
                self.act(kvs.rearrange("p h n -> p (h n)"), self.ps[1][:, :], AF.Copy, [self.pb[1]], [kvsb])
                self.ts(DVE, kf[:, :, 0:64], kvs[:, :, 0:64], rkv[:, t:t + 1], None, ALU.mult, None,
                        [kvsb, stb], [kfb])
                self.cp(DVE, kf[:, :, 64:96], krr[:, t, :].unsqueeze(1).to_broadcast([128, 4, 32]), [krb], [kfb])
                self.act(sq.rearrange("p h n -> p (h n)"), kf.rearrange("p h n -> p (h n)"), AF.Square, [kfb], [sqb])
                self.reduce(s4[:, 4:8], sq, [sqb], [s4b])
                self.act(s4[:, 4:8], s4[:, 4:8], AF.Sqrt, [s4b, self.cb], [s4b], scale=1.0 / 96, bias=self.epsc[:, 0:1])
                self.recip(s4[:, 4:8], s4[:, 4:8], [s4b], [s4b])
                self.tt(DVE, kn, kf, s4[:, 4:8].unsqueeze(2).to_broadcast([128, 4, 96]), ALU.mult, [kfb, s4b], [knb])
                for h in range(4):
                    self.tr(tpq[0:96, 4 + h, :], kn[:, h, :], self.identb, [knb, self.cb], [self.pb[2]])
                self.act(KT[0:96, :, tsl], tpq[0:96, 4:8, :], AF.Copy, [self.pb[2]], [ktb, self.HTb])
                self.act(va[:, t, :, 0:64], kvs[:, :, 64:128], AF.Identity, [kvsb, stb], [vab], scale=rkv[:, t:t + 1])
            self.chk(5)
            its = [(h, qg, kt) for h in range(4) for qg in range(4) for kt in range(NT)]
            sbanks = (3, 4, 6)
            obanks = (5, 2)
            LA = 2

            def qk_exp(i):
                h, qg, kt = its[i]
                sbk = sbanks[i % 3]
                ps_ = i % 4
                self.mm(self.ps[sbk][:, :], KT[0:96, h, kt * 128:(kt + 1) * 128], QT[0:96, h, qg * 512:(qg + 1) * 512],
                        True, True, [ktb, qtb], [self.pb[sbk]])
                self.act(pt[ps_], self.ps[sbk][:, :], AF.Exp, [self.pb[sbk]], [ptb[ps_]])

            def pv(i):
                h, qg, kt = its[i]
                ps_ = i % 4
                ob = obanks[(h * 4 + qg) % 2]
                qsl = slice(qg * 512, (qg + 1) * 512)
                self.mm(self.ps[ob][0:65, :], va[:, kt, h, :], pt[ps_], kt == 0, kt == NT - 1,
                        [vab, ptb[ps_]], [self.pb[ob]])
                if kt == NT - 1:
                    self.act(osb_[0:65, :], self.ps[ob][0:65, :], AF.Copy, [self.pb[ob]], [osb])
                    self.recip(rrow[64:65, 0:512], osb_[64:65, :], [osb], [rrb])
                    self.mm(self.ps[7][0:64, :], self.ones[64:65, 0:64], rrow[64:65, 0:512], True, True,
                            [rrb, self.cb], [self.pb[7]])
                    self.tt(DVE, ont[0:64, h, qsl], osb_[0:64, :], self.ps[7][0:64, :], ALU.mult,
                            [osb, self.pb[7]], [ontb])

            for step in range(len(its) + LA):
                if step < len(its):
                    qk_exp(step)
                if step >= LA:
                    pv(step - LA)
            self.out_proj(ont, ontb, [], wo, wob, ytmp, ytb)

    def dil_layer(self):
        R = self.R
        R.reset()
        P = self.P
        tab = R.take(48 * 4)
        ohs = R.take(1152 * 4)
        tbx = R.take(48 * 128 * 4).rearrange("p (g n) -> p g n", g=48)
        rbs = [R.take(384 * 2, BF16) for _ in range(2)]
        tabb, ohb, tbxb = Buf("tab"), Buf("oh"), Buf("tbx")
        rbb = [Buf("rb0"), Buf("rb1")]
        zb = Buf("Z")
        gcb = Buf("gcols")
        self.memset(DVE, tab[32:64, 0:48], -30000.0, [tabb])
        self.dma(SP, tab[0:32, 0:48], self.rel_bias[:, :], [], [tabb], tabb)
        self.dma(SP, ohs[0:33, :], self.oh[:, :], [], [ohb], ohb)
        self.cp(DVE, tbx[0:33], tab[0:33, 0:48].unsqueeze(2).to_broadcast([33, 48, 128]), [tabb], [tbxb])
        for gh in range(48):
            g = gh // 16
            s = gh % 2
            self.mm(self.ps[s][:, 0:384], tbx[0:33, gh, :], ohs[0:33, g * 384:(g + 1) * 384], True, True,
                    [tbxb, ohb], [self.pb[s]])
            self.act(rbs[s], self.ps[s][:, 0:384], AF.Exp, [self.pb[s]], [rbb[s]])
            zw = bass.AP(self.zscr, gh * 128 * 640, [[641, 128], [1, 384]])
            self.dma(SP, zw, rbs[s], [rbb[s]], [zb], zb)
        P.barrier()

        R.reset()
        gcols2 = R.take(32)
        et = R.take(12 * 256 * 2, BF16).rearrange("p (g n) -> p g n", g=12)
        etb = Buf("et")
        wd = R.take(8 * 768 * 2, BF16).rearrange("p (k n) -> p k n", k=8)
        wdb = Buf("wd")
        qk_off = R.cur
        QT = R.take(2 * 2048 * 2, BF16).rearrange("p (a n) -> p a n", a=2)
        KT = R.take(2 * 2048 * 2, BF16).rearrange("p (a n) -> p a n", a=2)
        qtb, ktb = Buf("QT1"), Buf("KT1")
        ontv = self.view(qk_off, 16384, BF16).rearrange("p (h n) -> p h n", h=4)
        va = R.take(16 * 4 * 65 * 2, BF16).rearrange("p (t h n) -> p t h n", t=NT, h=4)
        vab = Buf("va1")
        acc = R.take(4 * 2048 * 4).rearrange("p (h n) -> p h n", h=4)
        accb = Buf("acc")
        wo = R.take(4 * 1024 * 2, BF16).rearrange("p (h n) -> p h n", h=4)
        wob = Buf("wo1")
        qs = R.take(512 * 4)
        sq = R.take(512 * 4)
        qkn = R.take(512 * 2, BF16)
        s8 = R.take(32)
        qsb, sqb, qknb, s8b = Buf("qs1"), Buf("sq1"), Buf("qkn"), Buf("s8")
        pt = [R.take(256 * 2, BF16) for _ in range(4)]
        ptb = [Buf("pt1_%d" % i) for i in range(4)]
        rrow = R.take(512 * 4)
        rrb = Buf("rrow1")
        ytmp = [R.take(4096) for _ in range(2)]
        ytb = [Buf("yt1_0"), Buf("yt1_1")]
        ontb = Buf("ont1")
        grow2 = acc[:, 0, :]
        g42 = grow2[:, 0:768].rearrange("p (a b n) -> p a b n", a=6, b=2)
        for rep in range(2):
            self.dma(SP, g42[0:1, 0:3, rep, :], self.dil_g_q[0:1, :].rearrange("o (g n) -> o g n", g=3), [], [accb], accb)
            self.dma(SP, g42[0:1, 3:6, rep, :], self.dil_g_k[0:1, :].rearrange("o (g n) -> o g n", g=3), [], [accb], accb)
        self.ts(DVE, grow2[0:1, 0:384], grow2[0:1, 0:384], 0.125, None, ALU.mult, None, [accb], [accb])
        self.row_to_cols(self.ps[6], self.pb[6], grow2, accb, 768)
        self.cp(DVE, gcols2[:, 0:6], self.ps[6][:, 0:6], [self.pb[6]], [gcb])
        self.memset(POOL, va[:, :, :, 64:65], 1.0, [vab])
        Wd = self.dil_w_in.rearrange("(k p) n -> p k n", p=128)
        tpq = self.psb[2].rearrange("p (a n) -> p a n", a=8)
        wins = [_windows(d) for (_, d) in DIL]
        pcnt = 0
        ocnt = 0
        for hb in range(4):
            self.dma(POOL, wo[0:64, :, :], self.dil_w_o[hb * 256:(hb + 1) * 256, :].rearrange("(h d) n -> d h n", d=64),
                     [], [wob], wob)
            for g in range(3):
                for hh in range(4):
                    gh = g * 16 + hb * 4 + hh
                    zr = bass.AP(self.zscr, gh * 128 * 640 + 127, [[640, 128], [1, 256]])
                    self.dma(SP, et[:, g * 4 + hh, :], zr, [zb], [etb], etb)
            for g, (_, d) in enumerate(DIL):
                L = S // d
                for part in range(3):
                    c0 = g * 3072 + part * 1024 + hb * 256
                    for k in range(8):
                        self.dma(POOL, wd[:, k, part * 256:(part + 1) * 256], Wd[:, k, c0:c0 + 256], [], [wdb], wdb)
                for ct in range(NT):
                    r = (128 * ct) // L
                    j0 = (128 * ct) % L
                    tok0 = r + d * j0
                    csl = slice(ct * 128, (ct + 1) * 128)
                    for k in range(8):
                        lhs = self.HT[:, k, tok0:tok0 + 127 * d + 1:d] if d > 1 else self.HT[:, k, tok0:tok0 + 128]
                        self.mm(self.ps[0][:, :], lhs, wd[:, k, 0:512], k == 0, k == 7, [self.HTb, wdb], [self.pb[0]])
                    for k in range(8):
                        lhs = self.HT[:, k, tok0:tok0 + 127 * d + 1:d] if d > 1 else self.HT[:, k, tok0:tok0 + 128]
                        self.mm(self.ps[1][:, 0:256], lhs, wd[:, k, 512:768], k == 0, k == 7, [self.HTb, wdb],
                                [self.pb[1]])
                    self.act(qs, self.ps[0][:, :], AF.Copy, [self.pb[0]], [qsb])
                    self.act(sq, qs, AF.Square, [qsb], [sqb])
                    self.reduce(s8[:, 0:8], sq.rearrange("p (h n) -> p h n", h=8), [sqb], [s8b])
                    self.act(s8[:, 0:8], s8[:, 0:8], AF.Sqrt, [s8b, self.cb], [s8b], scale=1.0 / 64, bias=self.epsc[:, 0:1])
                    self.recip(s8[:, 0:8], s8[:, 0:8], [s8b], [s8b])
                    self.tt(DVE, qkn.rearrange("p (h n) -> p h n", h=8), qs.rearrange("p (h n) -> p h n", h=8),
                            s8[:, 0:8].unsqueeze(2).to_broadcast([128, 8, 64]), ALU.mult, [qsb, s8b], [qknb])
                    for a in range(4):
                        self.tr(tpq[:, a, :], qkn[:, a * 128:(a + 1) * 128], self.identb, [qknb, self.cb], [self.pb[2]])
                    self.act(QT[:, :, csl], tpq[:, 0:2, :], AF.Identity, [self.pb[2], gcb], [qtb, ontb],
                             scale=gcols2[:, g:g + 1])
                    self.act(KT[:, :, csl], tpq[:, 2:4, :], AF.Identity, [self.pb[2], gcb], [ktb, ontb],
                             scale=gcols2[:, 3 + g:4 + g])
                    self.cp(DVE, va[:, ct, :, 0:64], self.ps[1][:, 0:256].rearrange("p (h n) -> p h n", h=4),
                            [self.pb[1]], [vab])
                its = []
                for hh in range(4):
                    for bank in range(4):
                        ents = wins[g][bank]
                        for ei, ent in enumerate(ents):
                            its.append((hh, bank, ei, len(ents), ent))
                sbanks = (3, 4, 7)
                LA = 2

                def qk_exp(i, g=g, its=its):
                    hh, bank, ei, ne, (kt, qa, qb, eoff) = its[i]
                    pr, hf = hh // 2, hh % 2
                    psl = slice(64 * hf, 64 * hf + 64)
                    n = qb - qa
                    sbk = sbanks[i % 3]
                    ps_ = i % 4
                    self.mm(self.ps[sbk][:, 0:n], KT[psl, pr, kt * 128:(kt + 1) * 128], QT[psl, pr, qa:qb],
                            True, True, [ktb, qtb], [self.pb[sbk]])
                    self.act(pt[ps_][:, 0:n], self.ps[sbk][:, 0:n], AF.Exp, [self.pb[sbk]], [ptb[ps_]])
                    self.tt(DVE, pt[ps_][:, 0:n], pt[ps_][:, 0:n], et[:, g * 4 + hh, eoff:eoff + n], ALU.mult,
                            [ptb[ps_], etb], [ptb[ps_]])

                def pv(i, g=g, d=d, its=its):
                    hh, bank, ei, ne, (kt, qa, qb, eoff) = its[i]
                    n = qb - qa
                    ps_ = i % 4
                    ob = 5 + (hh * 4 + bank) % 2
                    accv = acc[0:65, hh, :]
                    self.mm(self.ps[ob][0:65, qa - bank * 512:qb - bank * 512], va[:, kt, hh, :], pt[ps_][:, 0:n],
                            ei == 0, ei == ne - 1, [vab, ptb[ps_]], [self.pb[ob]], skip=True)
                    if ei == ne - 1:
                        if d == 1:
                            dst = accv[:, bank * 512:(bank + 1) * 512]
                            src = self.ps[ob][0:65, :]
                        elif d == 4:
                            dst = accv[:, bank:2048:4]
                            src = self.ps[ob][0:65, :]
                        else:
                            dst = accv.rearrange("p (j r) -> p r j", r=16)[:, 4 * bank:4 * bank + 4, :]
                            src = self.ps[ob][0:65, :].rearrange("p (r j) -> p r j", r=4)
                        if g == 0:
                            self.act(dst, src, AF.Copy, [self.pb[ob]], [accb])
                        else:
                            self.tt(DVE, dst, src, dst, ALU.add, [self.pb[ob], accb], [accb])

                for step in range(len(its) + LA):
                    if step < len(its):
                        qk_exp(step)
                    if step >= LA:
                        pv(step - LA)
            self.normalize_heads(lambda h, ch: (acc[0:64, h, ch * 512:(ch + 1) * 512],
                                                acc[64:65, h, ch * 512:(ch + 1) * 512]),
                                 [accb], ontv, ontb, [qtb, ktb], rrow, rrb)
            self.out_proj(ontv, ontb, [qtb, ktb], wo, wob, ytmp, ytb)


_CACHE = {}


def _program(layers):
    key = tuple(layers)
    if key not in _CACHE:
        _CACHE[key] = Builder(layers).build()
    return _CACHE[key]


def _in_maps(inp, x_override=None):
    f = lambda a: np.ascontiguousarray(np.asarray(a, dtype=np.float32))
    x = f(inp["x"]) if x_override is None else x_override
    pos = np.ascontiguousarray(np.asarray(inp["positions"], dtype=np.int32))
    shared = {
        "ada_w": f(inp["ada_w"]), "ada_b": f(inp["ada_b"]),
        "norm_mix": f(inp["norm_mix"]), "norm_mlp": f(inp["norm_mlp"]),
        "mlp_w1": f(inp["mlp_w1"]), "mlp_w2": f(inp["mlp_w2"]),
        "mla_w_in": f(inp["mla_w_in"])[0],
        "mla_g_lat": np.ascontiguousarray(np.concatenate([f(inp["mla_g_qa"])[0], f(inp["mla_g_kva"])[0]])[None, :]),
        "mla_w_qb": f(inp["mla_w_qb"])[0], "mla_w_kvb": f(inp["mla_w_kvb"])[0],
        "mla_g_q": f(inp["mla_g_q"]), "mla_g_k": f(inp["mla_g_k"]),
        "mla_w_o": f(inp["mla_w_o"])[0],
        "dil_w_in": f(inp["dil_w_in"])[0],
        "dil_g_q": f(inp["dil_g_q"]).reshape(1, 192), "dil_g_k": f(inp["dil_g_k"]).reshape(1, 192),
        "dil_w_o": f(inp["dil_w_o"])[0],
        "rel_bias": f(inp["rel_bias"]),
        "oh": _onehot_const(), "invf": _invf_const(),
    }
    maps = []
    for b in range(x.shape[0]):
        m = dict(shared)
        m["x"] = np.ascontiguousarray(x[b])
        m["c"] = np.ascontiguousarray(f(inp["c"])[b:b + 1])
        m["pos"] = np.ascontiguousarray(pos[b].reshape(16, 128))
        maps.append(m)
    return maps


LAUNCHES = ((0, 1),)


def kernel(**inputs):
    x = None
    for layers in LAUNCHES:
        nc = _program(layers)
        maps = _in_maps(inputs, x)
        res = run_bass_kernel_spmd(nc, maps, core_ids=list(range(len(maps))))
        x = np.stack([np.asarray(r["out"], dtype=np.float32) for r in res.results], axis=0)
    return x
```

```python
import contextlib
import math
import numpy as np
import concourse.bass as bass
import concourse.mybir as mybir
from concourse.bass_utils import run_bass_kernel_spmd

F32 = mybir.dt.float32
BF16 = mybir.dt.bfloat16
I32 = mybir.dt.int32
AF = mybir.ActivationFunctionType
ALU = mybir.AluOpType
AX = mybir.AxisListType

PE, ACT, DVE, POOL, SP = "pe", "act", "dve", "pool", "sp"
ENGS = [PE, ACT, DVE, POOL, SP]

S = 2048
D = 1024
NT = 16
EPS = 1e-6
DIL = ((128, 1), (512, 4), (2048, 16))


class Buf:
    __slots__ = ("name", "last_write", "readers", "dma_sem", "dma_cnt", "last_dma", "excl")

    _n = [0]

    def __init__(self, name, excl=False):
        self.excl = excl
        Buf._n[0] += 1
        self.name = "%s_%d" % (name, Buf._n[0])
        self.last_write = None
        self.readers = []
        self.dma_sem = None
        self.dma_cnt = 0
        self.last_dma = None


class Op:
    __slots__ = ("eng", "fn", "deps", "is_dma", "needs_inc", "ordinal", "dma_buf", "dma_val")

    def __init__(self, eng, fn, is_dma):
        self.eng = eng
        self.fn = fn
        self.deps = []
        self.is_dma = is_dma
        self.needs_inc = False
        self.ordinal = None
        self.dma_buf = None
        self.dma_val = None


class Prog:
    def __init__(self, nc):
        self.nc = nc
        self.ops = {e: [] for e in ENGS}
        self.dma_bufs = []
        self.fence = {e: [] for e in ENGS}

    def _add_dep(self, op, prod):
        if prod is None or prod is op:
            return
        if (not prod.is_dma) and prod.eng == op.eng and prod.eng == PE:
            return
        op.deps.append(prod)
        if not prod.is_dma:
            prod.needs_inc = True

    def op(self, eng, fn, reads=(), writes=(), dma_dst=None):
        o = Op(eng, fn, dma_dst is not None)
        if self.fence[eng]:
            for p in self.fence[eng]:
                if p.is_dma or p.eng != eng:
                    self._add_dep(o, p)
            self.fence[eng] = []
        for b in reads:
            self._add_dep(o, b.last_write)
            if b.excl:
                for r in b.readers:
                    if r.eng != eng:
                        self._add_dep(o, r)
        for b in writes:
            self._add_dep(o, b.last_write)
            for r in b.readers:
                self._add_dep(o, r)
        if dma_dst is not None:
            if dma_dst.dma_sem is None:
                dma_dst.dma_sem = "pending"
                self.dma_bufs.append(dma_dst)
            dma_dst.dma_cnt += 1
            o.dma_buf = dma_dst
            o.dma_val = 16 * dma_dst.dma_cnt
            dma_dst.last_dma = o
        for b in reads:
            b.readers.append(o)
        for b in writes:
            b.last_write = o
            b.readers = []
        self.ops[eng].append(o)
        return o

    def barrier(self):
        f = []
        for e in ENGS:
            last = None
            for o in reversed(self.ops[e]):
                if not o.is_dma:
                    last = o
                    break
            if last is not None:
                f.append(last)
        for b in self.dma_bufs:
            if b.last_dma is not None:
                f.append(b.last_dma)
        self.fence = {e: list(f) for e in ENGS}

    def emit(self, final_waits=()):
        nc = self.nc
        with contextlib.ExitStack() as st:
            sems = {e: st.enter_context(nc.semaphore("s_" + e)) for e in ENGS}
            for b in self.dma_bufs:
                b.dma_sem = st.enter_context(nc.semaphore("d_" + b.name))
            for e in ENGS:
                n = 0
                for o in self.ops[e]:
                    if o.needs_inc and not o.is_dma:
                        n += 1
                        o.ordinal = n
            block = st.enter_context(nc.Block())
            prog = self

            def run(e, h):
                waited = {}
                for o in prog.ops[e]:
                    for p in o.deps:
                        if p.is_dma:
                            s, v = p.dma_buf.dma_sem, p.dma_val
                        else:
                            s, v = sems[p.eng], p.ordinal
                        key = id(s)
                        if waited.get(key, 0) >= v:
                            continue
                        waited[key] = v
                        h.wait_ge(s, v)
                    ins = o.fn(h)
                    if o.is_dma:
                        ins.then_inc(o.dma_buf.dma_sem, 16)
                    elif o.needs_inc:
                        ins.then_inc(sems[e], 1)
                if e == SP:
                    for b in final_waits:
                        h.wait_ge(b.dma_sem, 16 * b.dma_cnt)

            @block.tensor
            def _(h):
                run(PE, h)

            @block.scalar
            def _(h):
                run(ACT, h)

            @block.vector
            def _(h):
                run(DVE, h)

            @block.gpsimd
            def _(h):
                run(POOL, h)

            @block.sync
            def _(h):
                run(SP, h)


def _t5_bucket_np(rel):
    n = np.abs(rel)
    thr = [15, 27, 50, 91, 166, 305, 559]
    mag = np.minimum(n, 8) + sum((n >= t).astype(np.int64) for t in thr)
    return (rel > 0) * 16 + mag


def _onehot_const():
    oh = np.zeros((33, 3, 384), np.float32)
    for g, (_, d) in enumerate(DIL):
        for m in range(384):
            j = 191 - m
            if abs(j) <= 64:
                oh[_t5_bucket_np(np.array([j * d]))[0], g, m] = 1.0
            else:
                oh[32, g, m] = 1.0
    return oh.reshape(33, 3 * 384)


def _invf_const():
    half = 16
    inv = 1.0 / (10000.0 ** (np.arange(half, dtype=np.float64) / half))
    return np.ascontiguousarray(np.broadcast_to((inv / (2 * math.pi)).astype(np.float32), (128, 16)))


def _windows(d):
    L = S // d
    nkt = L // 128
    banks = [[] for _ in range(4)]
    for r in range(d):
        for jt in range(nkt):
            lo = max(0, 128 * jt - 64)
            hi = min(L, 128 * jt + 192)
            a = r * L + lo
            b = r * L + hi
            while a < b:
                bank = a // 512
                e = min(b, (bank + 1) * 512)
                q_in = a - r * L
                eoff = q_in - (128 * jt - 64)
                banks[bank].append((r * nkt + jt, a, e, eoff))
                a = e
    return banks


class _Stop(Exception):
    pass


class Builder:
    def __init__(self, layers, stop=None, dbg=""):
        self.layers = tuple(layers)
        self.stop = stop
        self.dbg = dbg
        nc = self.nc = bass.Bass("TRN2", target_bir_lowering=False)
        self.P = Prog(nc)
        dt = nc.dram_tensor
        inp = lambda name, shape, ty=F32: dt(name, shape, ty, kind="ExternalInput").ap()
        self.x = inp("x", [S, D])
        self.c = inp("c", [1, D])
        self.pos = inp("pos", [16, 128], I32)
        self.ada_w = inp("ada_w", [2, D, 6 * D])
        self.ada_b = inp("ada_b", [2, 6 * D])
        self.norm_mix = inp("norm_mix", [2, D])
        self.norm_mlp = inp("norm_mlp", [2, D])
        self.w1 = inp("mlp_w1", [2, D, 4 * D])
        self.w2 = inp("mlp_w2", [2, 4 * D, D])
        self.mla_w_in = inp("mla_w_in", [D, 672])
        self.mla_g_lat = inp("mla_g_lat", [1, 640])
        self.mla_w_qb = inp("mla_w_qb", [384, 1536])
        self.mla_w_kvb = inp("mla_w_kvb", [256, 2048])
        self.mla_g_q = inp("mla_g_q", [1, 96])
        self.mla_g_k = inp("mla_g_k", [1, 96])
        self.mla_w_o = inp("mla_w_o", [D, D])
        self.dil_w_in = inp("dil_w_in", [D, 9216])
        self.dil_g_q = inp("dil_g_q", [1, 192])
        self.dil_g_k = inp("dil_g_k", [1, 192])
        self.dil_w_o = inp("dil_w_o", [D, D])
        self.rel_bias = inp("rel_bias", [32, 48])
        self.oh = inp("oh", [33, 1152])
        self.invf = inp("invf", [128, 16])
        self.out = dt("out", [S, D], F32, kind="ExternalOutput").ap()
        self.zscr = dt("zscr", [48 * 128 * 640], BF16, kind="Internal")

    def view(self, off, nbytes, dtype=F32, parts=128):
        assert off % 4 == 0 and nbytes % 4 == 0 and off + nbytes <= self.ARENA, (off, nbytes)
        a = self.arena[0:parts, off // 4:(off + nbytes) // 4]
        return a if dtype == F32 else a.bitcast(dtype)

    class Region:
        def __init__(self, b, start, end):
            self.b, self.start, self.end, self.cur = b, start, end, start

        def take(self, nbytes, dtype=F32):
            nbytes = (nbytes + 31) // 32 * 32
            assert self.cur + nbytes <= self.end, ("region overflow", self.cur, nbytes, self.end)
            v = self.b.view(self.cur, nbytes, dtype)
            self.cur += nbytes
            return v

        def reset(self):
            self.cur = self.start

    def mm(self, out, lhsT, rhs, start, stop, reads, writes, skip=False):
        self.P.op(PE, lambda e: e.matmul(out, lhsT=lhsT, rhs=rhs, start=start, stop=stop,
                                         skip_group_check=skip), reads, writes)

    def tr(self, out, in_, ident, reads, writes):
        self.P.op(PE, lambda e: e.transpose(out=out, in_=in_, identity=ident), reads, writes)

    def act(self, out, in_, func, reads, writes, scale=None, bias=None, accum=None):
        kw = {}
        if scale is not None:
            kw["scale"] = scale
        if bias is not None:
            kw["bias"] = bias
        if accum is not None:
            kw["accum_out"] = accum
        self.P.op(ACT, lambda e: e.activation(out=out, in_=in_, func=func, **kw), reads, writes)

    def tt(self, eng, out, in0, in1, op, reads, writes):
        self.P.op(eng, lambda e: e.tensor_tensor(out=out, in0=in0, in1=in1, op=op), reads, writes)

    def ts(self, eng, out, in0, s1, s2, op0, op1, reads, writes):
        if s2 is None:
            self.P.op(eng, lambda e: e.tensor_scalar(out=out, in0=in0, scalar1=s1, scalar2=None, op0=op0),
                      reads, writes)
        else:
            self.P.op(eng, lambda e: e.tensor_scalar(out=out, in0=in0, scalar1=s1, scalar2=s2, op0=op0, op1=op1),
                      reads, writes)

    def stt(self, eng, out, in0, scalar, in1, op0, op1, reads, writes):
        self.P.op(eng, lambda e: e.scalar_tensor_tensor(out=out, in0=in0, scalar=scalar, in1=in1, op0=op0, op1=op1),
                  reads, writes)

    def cp(self, eng, out, in_, reads, writes):
        self.P.op(eng, lambda e: e.tensor_copy(out=out, in_=in_), reads, writes)

    def memset(self, eng, ap, val, writes):
        self.P.op(eng, lambda e: e.memset(ap, val), (), writes)

    def recip(self, out, in_, reads, writes):
        self.P.op(DVE, lambda e: e.reciprocal(out=out, in_=in_), reads, writes)

    def reduce(self, out, in_, reads, writes):
        self.P.op(DVE, lambda e: e.tensor_reduce(out=out, in_=in_, axis=AX.X, op=ALU.add), reads, writes)

    def dma(self, eng, out, in_, reads, writes, dst):
        self.P.op(eng, lambda e: e.dma_start(out=out, in_=in_), reads, writes, dma_dst=dst)

    def row_to_cols(self, ps_ap, ps_buf, row_ap, row_buf, n, m=128, col0=0):
        for j in range(n // m):
            self.mm(ps_ap[0:m, col0 + j:col0 + j + 1], row_ap[0:1, j * m:(j + 1) * m], self.one[0:1, 0:1],
                    True, True, [row_buf, self.cb], [ps_buf])

    def build(self):
        nc = self.nc
        self.ARENA = 212800
        with contextlib.ExitStack() as st:
            self.arena = st.enter_context(nc.sbuf_tensor("arena", [128, self.ARENA // 4], F32))
            self.ps = [st.enter_context(nc.psum_tensor("ps%d" % i, [128, 512], F32)) for i in range(8)]
            self.pb = [Buf("ps%d" % i, excl=True) for i in range(8)]
            self.psb = [self.ps[i][:, :].bitcast(BF16) for i in range(8)]
            K = 1024
            self.X = self.view(0, 64 * K).rearrange("p (t n) -> p t n", t=NT)
            self.Xb = [Buf("X%d" % t) for t in range(NT)]
            self.HTr = Builder.Region(self, 64 * K, 96 * K)
            self.HT = self.view(64 * K, 32 * K, BF16).rearrange("p (c n) -> p c n", c=8)
            self.HTb = Buf("HT")
            self.gates = self.view(96 * K, 8 * K).rearrange("p (g n) -> p g n", g=2)
            self.gb = Buf("gates")
            cr = Builder.Region(self, 104 * K, 107 * K)
            self.identb = cr.take(256, BF16)
            self.identf = cr.take(512)
            self.ones = cr.take(512)
            self.one = self.ones
            self.cols = cr.take(128)
            self.epsc = cr.take(32)
            self.ss = cr.take(64)
            self.rstd = cr.take(64)
            self.condT = cr.take(32, BF16)
            self.cb = Buf("consts")
            self.colsb = Buf("cols")
            self.statb = Buf("stat")
            self.R = Builder.Region(self, 107 * K, self.ARENA)
            self.NS = Builder.Region(self, self.ARENA - 8192, self.ARENA)
            self.outb = Buf("out")

            try:
                self.setup()
                self.chk(1)
                for li in self.layers:
                    self.ada(li)
                    self.P.barrier()
                    self.chk(2)
                    self.norm_to_ht(0)
                    self.chk(3)
                    if li == 0:
                        self.mla_layer()
                    else:
                        self.dil_layer()
                    self.P.barrier()
                    self.chk(6)
                    self.norm_to_ht(16)
                    self.mlp(li)
                    self.P.barrier()
            except _Stop:
                self.P.barrier()
            for t in range(NT):
                self.dma(SP, self.out[t * 128:(t + 1) * 128, :], self.X[:, t, :], [self.Xb[t]], [self.outb], self.outb)
            self.P.emit(final_waits=[self.outb])
        return nc

    def chk(self, stage):
        if self.stop is not None and self.stop == stage:
            raise _Stop()

    def setup(self):
        for t in range(NT):
            self.dma(SP, self.X[:, t, :], self.x[t * 128:(t + 1) * 128, :], [], [self.Xb[t]], self.Xb[t])
        cb = self.cb
        self.memset(POOL, self.identf, 0.0, [cb])
        self.P.op(POOL, lambda e: e.affine_select(out=self.identf, in_=self.identf, pattern=[[-1, 128]],
                                                  compare_op=ALU.not_equal, fill=1.0, base=0,
                                                  channel_multiplier=1), [cb], [cb])
        self.cp(POOL, self.identb, self.identf, [cb], [cb])
        self.memset(POOL, self.ones, 1.0, [cb])
        self.memset(POOL, self.epsc, EPS, [cb])
        self.R.reset()
        crow = self.R.take(4096)
        crb = Buf("crow")
        self.dma(SP, crow[0:1, :], self.c[0:1, :], [], [crb], crb)
        self.act(crow[0:1, :], crow[0:1, :], AF.Silu, [crb], [crb])
        self.row_to_cols(self.ps[7], self.pb[7], crow, crb, 1024)
        self.cp(DVE, self.condT[:, 0:8], self.ps[7][:, 0:8], [self.pb[7]], [cb])
        self.P.barrier()

    def ada(self, li):
        self.R.reset()
        self.HTr.reset()
        modrow = self.HTr.take(24576)
        nmrow = self.HTr.take(8192)
        adab = self.R.take(24576)
        wa = [self.R.take(8 * 1536 * 2, BF16).rearrange("p (k n) -> p k n", k=8) for _ in range(2)]
        wab = [Buf("wa0"), Buf("wa1")]
        mb, nb, ab = Buf("modrow"), Buf("nmrow"), Buf("adab")
        self.dma(SP, adab[0:1, :], self.ada_b[li:li + 1, :], [], [ab], ab)
        self.dma(SP, nmrow[0:1, 0:1024], self.norm_mix[li:li + 1, :], [], [nb], nb)
        self.dma(SP, nmrow[0:1, 1024:2048], self.norm_mlp[li:li + 1, :], [], [nb], nb)
        aw = self.ada_w[li].rearrange("(k p) n -> p k n", p=128)
        for blk in range(4):
            s = blk % 2
            for k in range(8):
                self.dma(POOL, wa[s][:, k, :], aw[:, k, blk * 1536:(blk + 1) * 1536], [], [wab[s]], wab[s])
            for cg in range(3):
                pbank = 5 + (blk * 3 + cg) % 2
                for k in range(8):
                    self.mm(self.ps[pbank][0:1, :], self.condT[:, k:k + 1], wa[s][:, k, cg * 512:(cg + 1) * 512],
                            k == 0, k == 7, [wab[s], self.cb], [self.pb[pbank]])
                c0 = blk * 1536 + cg * 512
                self.tt(DVE, modrow[0:1, c0:c0 + 512], self.ps[pbank][0:1, :], adab[0:1, c0:c0 + 512], ALU.add,
                        [self.pb[pbank], ab], [mb])
        sl = lambda i: modrow[0:1, i * 1024:(i + 1) * 1024]
        self.stt(DVE, sl(1), sl(1), 1.0, nmrow[0:1, 0:1024], ALU.add, ALU.mult, [mb, nb], [mb])
        self.stt(DVE, sl(4), sl(4), 1.0, nmrow[0:1, 1024:2048], ALU.add, ALU.mult, [mb, nb], [mb])
        p7, pb7 = self.ps[7], self.pb[7]
        for j, src in enumerate((1, 0, 4, 3)):
            self.row_to_cols(p7, pb7, sl(src), mb, 1024, col0=8 * j)
        self.cp(DVE, self.cols[:, 0:32], p7[:, 0:32], [pb7], [self.colsb])
        for gi, src in enumerate((2, 5)):
            for hlf in range(2):
                pbank = 5 + hlf
                self.mm(self.ps[pbank][:, :], self.ones[0:1, 0:128], sl(src)[0:1, hlf * 512:(hlf + 1) * 512],
                        True, True, [mb, self.cb], [self.pb[pbank]])
                self.cp(DVE, self.gates[:, gi, hlf * 512:(hlf + 1) * 512], self.ps[pbank][:, :],
                        [self.pb[pbank]], [self.gb])

    def norm_to_ht(self, col0):
        self.NS.reset()
        junk = self.NS.take(4096)
        jb = Buf("junk")
        xn = [self.NS.take(2048, BF16) for _ in range(2)]
        xnb = [Buf("xn0"), Buf("xn1")]
        sb = self.statb
        self.memset(DVE, self.ss, 0.0, [sb])
        for t in range(NT):
            self.act(junk, self.X[:, t, :], AF.Square, [self.Xb[t]], [jb, sb], accum=self.ss[:, t:t + 1])
        self.act(self.rstd, self.ss, AF.Sqrt, [sb, self.cb], [sb], scale=1.0 / D, bias=self.epsc[:, 0:1])
        self.recip(self.rstd, self.rstd, [sb], [sb])
        tp = self.psb[2].rearrange("p (c n) -> p c n", c=8)
        tpb = self.pb[2]
        tp2 = self.psb[3].rearrange("p (c n) -> p c n", c=8)
        tps, tpbs = [tp, tp2], [self.pb[2], self.pb[3]]
        for t in range(NT):
            s = t % 2
            self.ts(DVE, xn[s], self.X[:, t, :], self.rstd[:, t:t + 1], None, ALU.mult, None,
                    [self.Xb[t], sb], [xnb[s]])
            for c in range(8):
                self.tr(tps[s][:, c, :], xn[s][:, c * 128:(c + 1) * 128], self.identb, [xnb[s], self.cb], [tpbs[s]])
            for c in range(8):
                o = self.HT[:, c, t * 128:(t + 1) * 128]
                sc = self.cols[:, col0 + c:col0 + c + 1]
                bi = self.cols[:, col0 + 8 + c:col0 + 8 + c + 1]
                if s == 0:
                    self.act(o, tps[s][:, c, :], AF.Identity, [tpbs[s], self.colsb], [self.HTb], scale=sc, bias=bi)
                else:
                    self.ts(DVE, o, tps[s][:, c, :], sc, bi, ALU.mult, ALU.add, [tpbs[s], self.colsb], [self.HTb])

    def resid_update(self, t, gi, banks, ytmp, ytb):
        for hlf in range(2):
            self.tt(DVE, ytmp[:, hlf * 512:(hlf + 1) * 512], self.ps[banks[hlf]][:, :],
                    self.gates[:, gi, hlf * 512:(hlf + 1) * 512], ALU.mult, [self.pb[banks[hlf]], self.gb], [ytb])
        self.tt(POOL, self.X[:, t, :], self.X[:, t, :], ytmp, ALU.add, [ytb, self.Xb[t]], [self.Xb[t]])

    def mlp(self, li):
        self.R.reset()
        R = self.R
        w1 = [R.take(8 * 1024 * 2, BF16).rearrange("p (k n) -> p k n", k=8) for _ in range(2)]
        w2 = [R.take(8 * 1024 * 2, BF16).rearrange("p (k n) -> p k n", k=8) for _ in range(2)]
        w1b = [Buf("w1_0"), Buf("w1_1")]
        w2b = [Buf("w2_0"), Buf("w2_1")]
        hid = [R.take(8 * 512 * 2, BF16).rearrange("p (k n) -> p k n", k=8) for _ in range(2)]
        hidb = [Buf("hid0"), Buf("hid1")]
        rs = [R.take(2048) for _ in range(2)]
        rsb = [Buf("rs0"), Buf("rs1")]
        ytmp = [R.take(4096) for _ in range(2)]
        ytb = [Buf("yt0"), Buf("yt1")]
        W1 = self.w1[li].rearrange("(k p) n -> p k n", p=128)
        W2 = self.w2[li].rearrange("(k p) n -> p k n", p=128)
        cnt = 0
        ycnt = 0
        for qh in range(4):
            s = qh % 2
            for k in range(8):
                self.dma(POOL, w1[s][:, k, :], W1[:, k, qh * 1024:(qh + 1) * 1024], [], [w1b[s]], w1b[s])
            for k in range(8):
                self.dma(POOL, w2[s][:, k, :], W2[:, qh * 8 + k, :], [], [w2b[s]], w2b[s])
            for tg in range(4):
                hs = tg % 2
                for hc in range(8):
                    pbk = cnt % 2
                    cnt += 1
                    for k in range(8):
                        self.mm(self.ps[pbk][:, :], w1[s][:, k, hc * 128:(hc + 1) * 128],
                                self.HT[:, k, tg * 512:(tg + 1) * 512], k == 0, k == 7,
                                [w1b[s], self.HTb], [self.pb[pbk]])
                    self.act(rs[pbk], self.ps[pbk][:, :], AF.Relu, [self.pb[pbk]], [rsb[pbk]])
                    self.tt(DVE, hid[hs][:, hc, :], rs[pbk], rs[pbk], ALU.mult, [rsb[pbk]], [hidb[hs]])
                for tt_ in range(4):
                    t = tg * 4 + tt_
                    ys = ycnt % 2
                    ycnt += 1
                    banks = (2 + 2 * ys, 3 + 2 * ys)
                    for hlf in range(2):
                        for hc in range(8):
                            self.mm(self.ps[banks[hlf]][:, :], hid[hs][:, hc, tt_ * 128:(tt_ + 1) * 128],
                                    w2[s][:, hc, hlf * 512:(hlf + 1) * 512], hc == 0, hc == 7,
                                    [hidb[hs], w2b[s]], [self.pb[banks[hlf]]])
                    self.resid_update(t, 1, banks, ytmp[ys], ytb[ys])

    def out_proj(self, ont, ontb, extra, wo, wob, ytmp, ytb):
        for t in range(NT):
            ys = t % 2
            banks = (0, 1) if ys == 0 else (2, 3)
            for hlf in range(2):
                for h in range(4):
                    self.mm(self.ps[banks[hlf]][:, :], ont[0:64, h, t * 128:(t + 1) * 128],
                            wo[0:64, h, hlf * 512:(hlf + 1) * 512], h == 0, h == 3,
                            [ontb, wob] + extra, [self.pb[banks[hlf]]])
            self.resid_update(t, 0, banks, ytmp[ys], ytb[ys])

    def normalize_heads(self, src_fn, src_bufs, ont, ontb, extra_w, rrows, rrbs):
        items = [(h, ch) for h in range(4) for ch in range(4)]
        banks = (7, 3)

        def fin(i):
            h, ch = items[i]
            num, _ = src_fn(h, ch)
            bk = banks[i % 2]
            self.mm(self.ps[bk][0:64, :], self.ones[64:65, 0:64], rrows[i % 2][64:65, 0:512], True, True,
                    [rrbs[i % 2], self.cb], [self.pb[bk]])
            self.tt(DVE, ont[0:64, h, ch * 512:(ch + 1) * 512], num, self.ps[bk][0:64, :], ALU.mult,
                    src_bufs + [self.pb[bk]], [ontb] + extra_w)

        for i, (h, ch) in enumerate(items):
            _, den = src_fn(h, ch)
            self.recip(rrows[i % 2][64:65, 0:512], den, src_bufs, [rrbs[i % 2]])
            if i > 0:
                fin(i - 1)
        fin(len(items) - 1)

    def mla_layer(self):
        R = self.R
        R.reset()
        P = self.P
        latt = R.take(5 * 2048 * 2, BF16).rearrange("p (j n) -> p j n", j=5)
        lattb = Buf("latt")
        epsq = R.take(64)
        rkv = R.take(64)
        ssq = R.take(64)
        sskv = R.take(64)
        kr = R.take(16 * 32 * 4).rearrange("p (t n) -> p t n", t=NT)
        krr = R.take(16 * 32 * 4).rearrange("p (t n) -> p t n", t=NT)
        cos = R.take(16 * 16 * 4).rearrange("p (t n) -> p t n", t=NT)
        sin = R.take(16 * 16 * 4).rearrange("p (t n) -> p t n", t=NT)
        gqk = R.take(32)
        stb, krb, csb, gqb = Buf("mstat"), Buf("kr"), Buf("cossin"), Buf("gqk")
        mark = R.cur
        win = R.take(8 * 672 * 2, BF16).rearrange("p (k n) -> p k n", k=8)
        winb = Buf("win")
        glat = R.take(640 * 4)
        glb = Buf("glat")
        grow = R.take(640 * 4)
        growb = Buf("grow")
        lat = [R.take(640 * 2, BF16) for _ in range(2)]
        latb = [Buf("lat0"), Buf("lat1")]
        junk = R.take(384 * 4)
        jb = Buf("junk")
        pi = R.take(512, I32)
        pf = R.take(512)
        posT = R.take(64)
        tq = R.take(1024).rearrange("p (t n) -> p t n", t=NT)
        ti = R.take(1024, I32).rearrange("p (t n) -> p t n", t=NT)
        tf = R.take(1024).rearrange("p (t n) -> p t n", t=NT)
        invf = R.take(64)
        pib, rb_ = Buf("pi"), Buf("ropetmp")
        Win = self.mla_w_in.rearrange("(k p) n -> p k n", p=128)
        winf = R.take(8 * 672 * 4).rearrange("p (k n) -> p k n", k=8)
        winfb = Buf("winf")
        for k in range(8):
            self.dma(SP, winf[:, k, :], Win[:, k, :], [], [winfb], winfb)
        self.cp(POOL, win[:, 0:4, :], winf[:, 0:4, :], [winfb], [winb])
        self.cp(DVE, win[:, 4:8, :], winf[:, 4:8, :], [winfb], [winb])
        self.dma(SP, grow[0:1, :], self.mla_g_lat[0:1, :], [], [growb], growb)
        self.dma(SP, pi[0:16, :], self.pos[:, :], [], [pib], pib)
        self.dma(SP, invf, self.invf[:, :], [], [rb_], rb_)
        for (a, b) in ((0, 512), (512, 640)):
            self.mm(self.ps[6][:, 0:b - a], self.ones[0:1, 0:128], grow[0:1, a:b], True, True,
                    [growb, self.cb], [self.pb[6]])
            self.cp(DVE, glat[:, a:b], self.ps[6][:, 0:b - a], [self.pb[6]], [glb])
        self.chk(41)
        self.dma(SP, grow[0:1, 0:96], self.mla_g_q[0:1, :], [glb], [growb], growb)
        self.dma(SP, grow[0:1, 128:224], self.mla_g_k[0:1, :], [glb], [growb], growb)
        self.stt(DVE, grow[0:1, 0:96], grow[0:1, 0:96], 96.0 ** -0.5, grow[0:1, 128:224], ALU.mult, ALU.mult,
                 [growb], [growb])
        self.row_to_cols(self.ps[6], self.pb[6], grow, growb, 96, m=96)
        self.cp(DVE, gqk[0:96, 0:1], self.ps[6][0:96, 0:1], [self.pb[6]], [gqb])
        self.chk(42)
        self.cp(DVE, pf[0:16, :], pi[0:16, :], [pib], [rb_])
        self.tr(self.ps[6][:, 0:16], pf[0:16, :], self.identf[0:16, 0:16], [rb_, self.cb], [self.pb[6]])
        self.cp(DVE, posT[:, 0:16], self.ps[6][:, 0:16], [self.pb[6]], [rb_])
        for t in range(NT):
            self.ts(DVE, tq[:, t, :], invf[:, 0:16], posT[:, t:t + 1], None, ALU.mult, None, [rb_], [rb_])
        for (dst, shift) in ((sin, 0.0), (cos, 0.25)):
            if shift != 0.0:
                self.ts(DVE, tq, tq, shift, None, ALU.add, None, [rb_], [rb_])
            self.cp(DVE, ti, tq, [rb_], [rb_])
            self.cp(DVE, tf, ti, [rb_], [rb_])
            self.tt(DVE, tf, tq, tf, ALU.subtract, [rb_], [rb_])
            self.act(dst, tf, AF.Sin, [rb_], [csb], scale=2 * math.pi * (1 - 1e-6))
        self.chk(43)
        self.memset(DVE, ssq, 0.0, [stb])
        self.memset(DVE, sskv, 0.0, [stb])
        self.memset(DVE, rkv, 0.0, [stb])
        tpl = [self.psb[2], self.psb[3]]
        for t in range(NT):
            s = t % 2
            b0, b1 = (0, 1) if s == 0 else (4, 5)
            for k in range(8):
                self.mm(self.ps[b0][:, :], self.HT[:, k, t * 128:(t + 1) * 128], win[:, k, 0:512], k == 0, k == 7,
                        [self.HTb, winb], [self.pb[b0]])
            for k in range(8):
                self.mm(self.ps[b1][:, 0:160], self.HT[:, k, t * 128:(t + 1) * 128], win[:, k, 512:672], k == 0,
                        k == 7, [self.HTb, winb], [self.pb[b1]])
            if "Q" in self.dbg:
                continue
            self.act(junk[:, 0:384], self.ps[b0][:, 0:384], AF.Square, [self.pb[b0]], [jb, stb], accum=ssq[:, t:t + 1])
            self.tt(DVE, lat[s][:, 0:512], self.ps[b0][:, :], glat[:, 0:512], ALU.mult, [self.pb[b0], glb], [latb[s]])
            self.act(junk[:, 0:128], self.ps[b0][:, 384:512], AF.Square, [self.pb[b0]], [jb, stb],
                     accum=sskv[:, t:t + 1])
            self.act(junk[:, 128:256], self.ps[b1][:, 0:128], AF.Square, [self.pb[b1]], [jb, stb],
                     accum=rkv[:, t:t + 1])
            self.tt(DVE, lat[s][:, 512:640], self.ps[b1][:, 0:128], glat[:, 512:640], ALU.mult,
                    [self.pb[b1], glb], [latb[s]])
            self.cp(DVE, kr[:, t, :], self.ps[b1][:, 128:160], [self.pb[b1]], [krb])
            if "T" in self.dbg:
                continue
            for j in range(5):
                self.tr(tpl[s][:, j * 128:(j + 1) * 128], lat[s][:, j * 128:(j + 1) * 128], self.identb,
                        [latb[s], self.cb], [self.pb[2 + s]])
            if "C" in self.dbg:
                continue
            self.act(latt[:, :, t * 128:(t + 1) * 128],
                     tpl[s][:, 0:640].rearrange("p (j n) -> p j n", j=5), AF.Copy, [self.pb[2 + s]], [lattb])
        self.chk(44)
        self.tt(DVE, sskv, sskv, rkv, ALU.add, [stb], [stb])
        self.act(rkv, sskv, AF.Sqrt, [stb, self.cb], [stb], scale=1.0 / 256, bias=self.epsc[:, 0:1])
        self.recip(rkv, rkv, [stb], [stb])
        self.ts(DVE, epsq, ssq, EPS / 384, EPS * EPS, ALU.mult, ALU.add, [stb], [stb])
        tmp = tq
        tmp2 = tf
        x1, x2 = kr[:, :, 0:16], kr[:, :, 16:32]
        self.tt(DVE, tmp, x1, cos, ALU.mult, [krb, csb], [rb_])
        self.tt(DVE, tmp2, x2, sin, ALU.mult, [krb, csb], [rb_])
        self.tt(DVE, krr[:, :, 0:16], tmp, tmp2, ALU.subtract, [rb_], [krb])
        self.tt(DVE, tmp, x2, cos, ALU.mult, [krb, csb], [rb_])
        self.tt(DVE, tmp2, x1, sin, ALU.mult, [krb, csb], [rb_])
        self.tt(DVE, krr[:, :, 16:32], tmp, tmp2, ALU.add, [rb_], [krb])
        P.barrier()
        self.chk(4)

        R.cur = mark
        wq = R.take(3 * 1536 * 2, BF16).rearrange("p (k n) -> p k n", k=3)
        wkv = R.take(2 * 2048 * 2, BF16).rearrange("p (k n) -> p k n", k=2)
        wqb, wkvb = Buf("wq"), Buf("wkv")
        Wq = self.mla_w_qb.rearrange("(k p) n -> p k n", p=128)
        Wkv = self.mla_w_kvb.rearrange("(k p) n -> p k n", p=128)
        for k in range(3):
            self.dma(POOL, wq[:, k, :], Wq[:, k, :], [], [wqb], wqb)
        for k in range(2):
            self.dma(POOL, wkv[:, k, :], Wkv[:, k, :], [], [wkvb], wkvb)
        va = R.take(16 * 4 * 65 * 2, BF16).rearrange("p (t h n) -> p t h n", t=NT, h=4)
        vab = Buf("va")
        ont = R.take(4 * 2048 * 2, BF16).rearrange("p (h n) -> p h n", h=4)
        ontb = Buf("ont")
        wo = R.take(4 * 1024 * 2, BF16).rearrange("p (h n) -> p h n", h=4)
        wob = Buf("wo")
        qkf = R.take(768 * 4).rearrange("p (h n) -> p h n", h=8)
        sq_off = R.cur
        sq = R.take(768 * 4).rearrange("p (h n) -> p h n", h=8)
        kvs = R.take(512 * 4).rearrange("p (h n) -> p h n", h=4)
        qkn = R.take(768 * 2, BF16).rearrange("p (h n) -> p h n", h=8)
        rt = R.take(4 * 4 * 16 * 4).rearrange("p (a h n) -> p a h n", a=4, h=4)
        s8 = R.take(64)
        qkfb, sqb, kvsb, qknb, rtb, s8b = (Buf(n) for n in ("qkf", "sq", "kvs", "qkn", "rt", "s8"))
        pt = [R.take(512 * 2, BF16) for _ in range(4)]
        ptb = [Buf("pt%d" % i) for i in range(4)]
        osb_ = R.take(512 * 4)
        osb = Buf("os")
        rrow = self.view(sq_off, 2048)
        rrb = sqb
        ytmp = [R.take(4096) for _ in range(2)]
        ytb = [Buf("yt0"), Buf("yt1")]
        QT = self.view(64 * 1024, 16 * 1024, BF16).rearrange("p (h n) -> p h n", h=4)
        KT = self.view(80 * 1024, 16 * 1024, BF16).rearrange("p (h n) -> p h n", h=4)
        qtb, ktb = Buf("QT"), Buf("KT")
        self.memset(POOL, va[:, :, :, 64:65], 1.0, [vab])
        tpq = self.psb[2].rearrange("p (h n) -> p h n", h=8)
        Wo = self.mla_w_o
        pcnt = 0
        for hg in range(4):
            self.dma(POOL, wo[0:64, :, :], Wo[hg * 256:(hg + 1) * 256, :].rearrange("(h d) n -> d h n", d=64),
                     [], [wob], wob)
            def gen_mm(t, hg=hg):
                tsl = slice(t * 128, (t + 1) * 128)
                for k in range(3):
                    self.mm(self.ps[0][:, 0:384], latt[:, k, tsl], wq[:, k, hg * 384:(hg + 1) * 384], k == 0, k == 2,
                            [lattb, wqb], [self.pb[0]])
                for k in range(2):
                    self.mm(self.ps[1][:, :], latt[:, 3 + k, tsl], wkv[:, k, hg * 512:(hg + 1) * 512], k == 0, k == 1,
                            [lattb, wkvb], [self.pb[1]])

            def chain_a(t):
                qv = qkf[:, 0:4, :]
                kv_ = qkf[:, 4:8, :]
                self.act(qv, self.ps[0][:, 0:384].rearrange("p (h n) -> p h n", h=4), AF.Copy, [self.pb[0]], [qkfb])
                self.act(kvs.rearrange("p h n -> p (h n)"), self.ps[1][:, :], AF.Copy, [self.pb[1]], [kvsb])
                self.ts(DVE, kv_[:, :, 0:64], kvs[:, :, 0:64], rkv[:, t:t + 1], None, ALU.mult, None,
                        [kvsb, stb], [qkfb])
                self.cp(DVE, kv_[:, :, 64:96], krr[:, t, :].unsqueeze(1).to_broadcast([128, 4, 32]), [krb], [qkfb])
                cb_ = cos[:, t, :].unsqueeze(1).to_broadcast([128, 4, 16])
                sb_ = sin[:, t, :].unsqueeze(1).to_broadcast([128, 4, 16])
                x1, x2 = qv[:, :, 64:80], qv[:, :, 80:96]
                self.tt(DVE, rt[:, 0], x1, cb_, ALU.mult, [qkfb, csb], [rtb])
                self.tt(DVE, rt[:, 1], x2, sb_, ALU.mult, [qkfb, csb], [rtb])
                self.tt(DVE, rt[:, 2], x2, cb_, ALU.mult, [qkfb, csb], [rtb])
                self.tt(DVE, rt[:, 3], x1, sb_, ALU.mult, [qkfb, csb], [rtb])
                self.tt(DVE, x1, rt[:, 0], rt[:, 1], ALU.subtract, [rtb], [qkfb])
                self.tt(DVE, x2, rt[:, 2], rt[:, 3], ALU.add, [rtb], [qkfb])
                self.act(va[:, t, :, 0:64], kvs[:, :, 64:128], AF.Identity, [kvsb, stb], [vab], scale=rkv[:, t:t + 1])
                self.act(sq.rearrange("p h n -> p (h n)"), qkf.rearrange("p h n -> p (h n)"), AF.Square, [qkfb], [sqb])
                self.reduce(s8[:, 0:8], sq, [sqb], [s8b])
                self.act(s8[:, 0:4], s8[:, 0:4], AF.Sqrt, [s8b, stb], [s8b], scale=1.0 / 96, bias=epsq[:, t:t + 1])
                self.act(s8[:, 4:8], s8[:, 4:8], AF.Sqrt, [s8b, self.cb], [s8b], scale=1.0 / 96, bias=self.epsc[:, 0:1])
                self.recip(s8[:, 0:8], s8[:, 0:8], [s8b], [s8b])

            def chain_b(t):
                self.tt(DVE, qkn, qkf, s8[:, 0:8].unsqueeze(2).to_broadcast([128, 8, 96]), ALU.mult,
                        [qkfb, s8b], [qknb])

            def gen_tr(t):
                tsl = slice(t * 128, (t + 1) * 128)
                for h in range(8):
                    self.tr(tpq[0:96, h, :], qkn[:, h, :], self.identb, [qknb, self.cb], [self.pb[2]])
                self.act(QT[0:96, :, tsl], tpq[0:96, 0:4, :], AF.Identity, [self.pb[2], gqb], [qtb, self.HTb],
                         scale=gqk[0:96, 0:1])
                self.act(KT[0:96, :, tsl], tpq[0:96, 4:8, :], AF.Copy, [self.pb[2]], [ktb, self.HTb])

            for t in range(NT):
                gen_mm(t)
                chain_a(t)
                if t > 0:
                    gen_tr(t - 1)
                chain_b(t)
            gen_tr(NT - 1)
            self.chk(5)
            its = [(h, qg, kt) for h in range(4) for qg in range(4) for kt in range(NT)]
            sbanks = (3, 4, 6)
            obanks = (5, 2)
            LA = 2

            def qk_exp(i):
                h, qg, kt = its[i]
                sbk = sbanks[i % 3]
                ps_ = i % 4
                self.mm(self.ps[sbk][:, :], KT[0:96, h, kt * 128:(kt + 1) * 128], QT[0:96, h, qg * 512:(qg + 1) * 512],
                        True, True, [ktb, qtb], [self.pb[sbk]])
                self.act(pt[ps_], self.ps[sbk][:, :], AF.Exp, [self.pb[sbk]], [ptb[ps_]])

            def pv(i):
                h, qg, kt = its[i]
                ps_ = i % 4
                ob = obanks[(h * 4 + qg) % 2]
                qsl = slice(qg * 512, (qg + 1) * 512)
                self.mm(self.ps[ob][0:65, :], va[:, kt, h, :], pt[ps_], kt == 0, kt == NT - 1,
                        [vab, ptb[ps_]], [self.pb[ob]])
                if kt == NT - 1:
                    self.act(osb_[0:65, :], self.ps[ob][0:65, :], AF.Copy, [self.pb[ob]], [osb])
                    self.recip(rrow[64:65, 0:512], osb_[64:65, :], [osb], [rrb])
                    pending.append((i + 6, h, qsl))

            def finish(h, qsl):
                self.mm(self.ps[7][0:64, :], self.ones[64:65, 0:64], rrow[64:65, 0:512], True, True,
                        [rrb, self.cb], [self.pb[7]])
                self.tt(DVE, ont[0:64, h, qsl], osb_[0:64, :], self.ps[7][0:64, :], ALU.mult,
                        [osb, self.pb[7]], [ontb])

            pending = []
            for step in range(len(its) + LA):
                if step < len(its):
                    qk_exp(step)
                if step >= LA:
                    pv(step - LA)
                    while pending and pending[0][0] <= step - LA:
                        _, h_, qsl_ = pending.pop(0)
                        finish(h_, qsl_)
            for _, h_, qsl_ in pending:
                finish(h_, qsl_)
            self.out_proj(ont, ontb, [], wo, wob, ytmp, ytb)

    def dil_layer(self):
        R = self.R
        R.reset()
        P = self.P
        tab = R.take(48 * 4)
        ohs = R.take(1152 * 4)
        tbx = R.take(48 * 128 * 4).rearrange("p (g n) -> p g n", g=48)
        rbs = [R.take(384 * 2, BF16) for _ in range(2)]
        tabb, ohb, tbxb = Buf("tab"), Buf("oh"), Buf("tbx")
        rbb = [Buf("rb0"), Buf("rb1")]
        zb = Buf("Z")
        gcb = Buf("gcols")
        self.memset(DVE, tab[32:64, 0:48], -30000.0, [tabb])
        self.dma(SP, tab[0:32, 0:48], self.rel_bias[:, :], [], [tabb], tabb)
        self.dma(SP, ohs[0:33, :], self.oh[:, :], [], [ohb], ohb)
        self.cp(DVE, tbx[0:33], tab[0:33, 0:48].unsqueeze(2).to_broadcast([33, 48, 128]), [tabb], [tbxb])
        for gh in range(48):
            g = gh // 16
            s = gh % 2
            self.mm(self.ps[s][:, 0:384], tbx[0:33, gh, :], ohs[0:33, g * 384:(g + 1) * 384], True, True,
                    [tbxb, ohb], [self.pb[s]])
            self.act(rbs[s], self.ps[s][:, 0:384], AF.Exp, [self.pb[s]], [rbb[s]])
            zw = bass.AP(self.zscr, gh * 128 * 640, [[641, 128], [1, 384]])
            self.dma(SP, zw, rbs[s], [rbb[s]], [zb], zb)
        P.barrier()

        R.reset()
        gcols2 = R.take(32)
        et = R.take(12 * 256 * 2, BF16).rearrange("p (g n) -> p g n", g=12)
        etb = Buf("et")
        wd = R.take(8 * 768 * 2, BF16).rearrange("p (k n) -> p k n", k=8)
        wdb = Buf("wd")
        qk_off = R.cur
        QT = R.take(2 * 2048 * 2, BF16).rearrange("p (a n) -> p a n", a=2)
        KT = R.take(2 * 2048 * 2, BF16).rearrange("p (a n) -> p a n", a=2)
        qtb, ktb = Buf("QT1"), Buf("KT1")
        ontv = self.view(qk_off, 16384, BF16).rearrange("p (h n) -> p h n", h=4)
        va = R.take(16 * 4 * 65 * 2, BF16).rearrange("p (t h n) -> p t h n", t=NT, h=4)
        vab = Buf("va1")
        acc = R.take(4 * 2048 * 4).rearrange("p (h n) -> p h n", h=4)
        accb = Buf("acc")
        wo = R.take(4 * 1024 * 2, BF16).rearrange("p (h n) -> p h n", h=4)
        wob = Buf("wo1")
        qs_off = R.cur
        qs = R.take(512 * 4)
        sq = R.take(512 * 4)
        qkn = R.take(512 * 2, BF16)
        s8 = R.take(32)
        qsb, sqb, qknb, s8b = Buf("qs1"), Buf("sq1"), Buf("qkn"), Buf("s8")
        pt = [R.take(256 * 2, BF16) for _ in range(4)]
        ptb = [Buf("pt1_%d" % i) for i in range(4)]
        rrows = [self.view(qs_off, 2048), self.view(qs_off + 2048, 2048)]
        rrbs = [qsb, sqb]
        ytmp = [R.take(4096) for _ in range(2)]
        ytb = [Buf("yt1_0"), Buf("yt1_1")]
        ontb = Buf("ont1")
        grow2 = acc[:, 0, :]
        g42 = grow2[:, 0:768].rearrange("p (a b n) -> p a b n", a=6, b=2)
        for rep in range(2):
            self.dma(SP, g42[0:1, 0:3, rep, :], self.dil_g_q[0:1, :].rearrange("o (g n) -> o g n", g=3), [], [accb], accb)
            self.dma(SP, g42[0:1, 3:6, rep, :], self.dil_g_k[0:1, :].rearrange("o (g n) -> o g n", g=3), [], [accb], accb)
        self.ts(DVE, grow2[0:1, 0:384], grow2[0:1, 0:384], 0.125, None, ALU.mult, None, [accb], [accb])
        self.row_to_cols(self.ps[6], self.pb[6], grow2, accb, 768)
        self.cp(DVE, gcols2[:, 0:6], self.ps[6][:, 0:6], [self.pb[6]], [gcb])
        self.memset(POOL, va[:, :, :, 64:65], 1.0, [vab])
        Wd = self.dil_w_in.rearrange("(k p) n -> p k n", p=128)
        tpq = self.psb[2].rearrange("p (a n) -> p a n", a=8)
        wins = [_windows(d) for (_, d) in DIL]
        pcnt = 0
        ocnt = 0
        for hb in range(4):
            self.dma(POOL, wo[0:64, :, :], self.dil_w_o[hb * 256:(hb + 1) * 256, :].rearrange("(h d) n -> d h n", d=64),
                     [], [wob], wob)
            for g in range(3):
                for hh in range(4):
                    gh = g * 16 + hb * 4 + hh
                    zr = bass.AP(self.zscr, gh * 128 * 640 + 127, [[640, 128], [1, 256]])
                    self.dma(SP, et[:, g * 4 + hh, :], zr, [zb], [etb], etb)
            for g, (_, d) in enumerate(DIL):
                L = S // d
                for part in range(3):
                    c0 = g * 3072 + part * 1024 + hb * 256
                    for k in range(8):
                        self.dma(POOL, wd[:, k, part * 256:(part + 1) * 256], Wd[:, k, c0:c0 + 256], [], [wdb], wdb)
                def d_mm(ct, d=d, L=L):
                    r = (128 * ct) // L
                    j0 = (128 * ct) % L
                    tok0 = r + d * j0
                    for k in range(8):
                        lhs = self.HT[:, k, tok0:tok0 + 127 * d + 1:d] if d > 1 else self.HT[:, k, tok0:tok0 + 128]
                        self.mm(self.ps[0][:, :], lhs, wd[:, k, 0:512], k == 0, k == 7, [self.HTb, wdb], [self.pb[0]])
                    for k in range(8):
                        lhs = self.HT[:, k, tok0:tok0 + 127 * d + 1:d] if d > 1 else self.HT[:, k, tok0:tok0 + 128]
                        self.mm(self.ps[1][:, 0:256], lhs, wd[:, k, 512:768], k == 0, k == 7, [self.HTb, wdb],
                                [self.pb[1]])

                def d_chain_a(ct):
                    self.cp(DVE, va[:, ct, :, 0:64], self.ps[1][:, 0:256].rearrange("p (h n) -> p h n", h=4),
                            [self.pb[1]], [vab])
                    self.act(qs, self.ps[0][:, :], AF.Copy, [self.pb[0]], [qsb])
                    self.act(sq, qs, AF.Square, [qsb], [sqb])
                    self.reduce(s8[:, 0:8], sq.rearrange("p (h n) -> p h n", h=8), [sqb], [s8b])
                    self.act(s8[:, 0:8], s8[:, 0:8], AF.Sqrt, [s8b, self.cb], [s8b], scale=1.0 / 64, bias=self.epsc[:, 0:1])
                    self.recip(s8[:, 0:8], s8[:, 0:8], [s8b], [s8b])

                def d_chain_b(ct):
                    self.tt(DVE, qkn.rearrange("p (h n) -> p h n", h=8), qs.rearrange("p (h n) -> p h n", h=8),
                            s8[:, 0:8].unsqueeze(2).to_broadcast([128, 8, 64]), ALU.mult, [qsb, s8b], [qknb])

                def d_tr(ct, g=g):
                    csl = slice(ct * 128, (ct + 1) * 128)
                    for a_ in range(4):
                        self.tr(tpq[:, a_, :], qkn[:, a_ * 128:(a_ + 1) * 128], self.identb, [qknb, self.cb], [self.pb[2]])
                    self.act(QT[:, :, csl], tpq[:, 0:2, :], AF.Identity, [self.pb[2], gcb], [qtb, ontb],
                             scale=gcols2[:, g:g + 1])
                    self.act(KT[:, :, csl], tpq[:, 2:4, :], AF.Identity, [self.pb[2], gcb], [ktb, ontb],
                             scale=gcols2[:, 3 + g:4 + g])

                for ct in range(NT):
                    d_mm(ct)
                    d_chain_a(ct)
                    if ct > 0:
                        d_tr(ct - 1)
                    d_chain_b(ct)
                d_tr(NT - 1)
                self.chk(7)
                its = []
                for hh in range(4):
                    for bank in range(4):
                        ents = wins[g][bank]
                        for ei, ent in enumerate(ents):
                            its.append((hh, bank, ei, len(ents), ent))
                sbanks = (3, 4, 7)
                LA = 2

                def qk_exp(i, g=g, its=its):
                    hh, bank, ei, ne, (kt, qa, qb, eoff) = its[i]
                    pr, hf = hh // 2, hh % 2
                    psl = slice(64 * hf, 64 * hf + 64)
                    n = qb - qa
                    sbk = sbanks[i % 3]
                    ps_ = i % 4
                    self.mm(self.ps[sbk][:, 0:n], KT[psl, pr, kt * 128:(kt + 1) * 128], QT[psl, pr, qa:qb],
                            True, True, [ktb, qtb], [self.pb[sbk]])
                    self.act(pt[ps_][:, 0:n], self.ps[sbk][:, 0:n], AF.Exp, [self.pb[sbk]], [ptb[ps_]])
                    self.tt(DVE, pt[ps_][:, 0:n], pt[ps_][:, 0:n], et[:, g * 4 + hh, eoff:eoff + n], ALU.mult,
                            [ptb[ps_], etb], [ptb[ps_]])

                def pv(i, g=g, d=d, its=its):
                    hh, bank, ei, ne, (kt, qa, qb, eoff) = its[i]
                    n = qb - qa
                    ps_ = i % 4
                    ob = 5 + (hh * 4 + bank) % 2
                    accv = acc[0:65, hh, :]
                    self.mm(self.ps[ob][0:65, qa - bank * 512:qb - bank * 512], va[:, kt, hh, :], pt[ps_][:, 0:n],
                            ei == 0, ei == ne - 1, [vab, ptb[ps_]], [self.pb[ob]], skip=True)
                    if ei == ne - 1:
                        if d == 1:
                            dst = accv[:, bank * 512:(bank + 1) * 512]
                            src = self.ps[ob][0:65, :]
                        elif d == 4:
                            dst = accv[:, bank:2048:4]
                            src = self.ps[ob][0:65, :]
                        else:
                            dst = accv.rearrange("p (j r) -> p r j", r=16)[:, 4 * bank:4 * bank + 4, :]
                            src = self.ps[ob][0:65, :].rearrange("p (r j) -> p r j", r=4)
                        if g == 0:
                            self.act(dst, src, AF.Copy, [self.pb[ob]], [accb])
                        else:
                            self.tt(DVE, dst, src, dst, ALU.add, [self.pb[ob], accb], [accb])

                for step in range(len(its) + LA):
                    if step < len(its):
                        qk_exp(step)
                    if step >= LA:
                        pv(step - LA)
            self.normalize_heads(lambda h, ch: (acc[0:64, h, ch * 512:(ch + 1) * 512],
                                                acc[64:65, h, ch * 512:(ch + 1) * 512]),
                                 [accb], ontv, ontb, [qtb, ktb], rrows, rrbs)
            self.out_proj(ontv, ontb, [qtb, ktb], wo, wob, ytmp, ytb)


_CACHE = {}


def _program(layers):
    key = tuple(layers)
    if key not in _CACHE:
        _CACHE[key] = Builder(layers).build()
    return _CACHE[key]


def _in_maps(inp, x_override=None):
    f = lambda a: np.ascontiguousarray(np.asarray(a, dtype=np.float32))
    x = f(inp["x"]) if x_override is None else x_override
    pos = np.ascontiguousarray(np.asarray(inp["positions"], dtype=np.int32))
    shared = {
        "ada_w": f(inp["ada_w"]), "ada_b": f(inp["ada_b"]),
        "norm_mix": f(inp["norm_mix"]), "norm_mlp": f(inp["norm_mlp"]),
        "mlp_w1": f(inp["mlp_w1"]), "mlp_w2": f(inp["mlp_w2"]),
        "mla_w_in": f(inp["mla_w_in"])[0],
        "mla_g_lat": np.ascontiguousarray(np.concatenate([f(inp["mla_g_qa"])[0], f(inp["mla_g_kva"])[0]])[None, :]),
        "mla_w_qb": f(inp["mla_w_qb"])[0], "mla_w_kvb": f(inp["mla_w_kvb"])[0],
        "mla_g_q": f(inp["mla_g_q"]), "mla_g_k": f(inp["mla_g_k"]),
        "mla_w_o": f(inp["mla_w_o"])[0],
        "dil_w_in": f(inp["dil_w_in"])[0],
        "dil_g_q": f(inp["dil_g_q"]).reshape(1, 192), "dil_g_k": f(inp["dil_g_k"]).reshape(1, 192),
        "dil_w_o": f(inp["dil_w_o"])[0],
        "rel_bias": f(inp["rel_bias"]),
        "oh": _onehot_const(), "invf": _invf_const(),
    }
    maps = []
    for b in range(x.shape[0]):
        m = dict(shared)
        m["x"] = np.ascontiguousarray(x[b])
        m["c"] = np.ascontiguousarray(f(inp["c"])[b:b + 1])
        m["pos"] = np.ascontiguousarray(pos[b].reshape(16, 128))
        maps.append(m)
    return maps


LAUNCHES = ((0, 1),)


def kernel(**inputs):
    x = None
    for layers in LAUNCHES:
        nc = _program(layers)
        maps = _in_maps(inputs, x)
        res = run_bass_kernel_spmd(nc, maps, core_ids=list(range(len(maps))))
        x = np.stack([np.asarray(r["out"], dtype=np.float32) for r in res.results], axis=0)
    return x
```
